# Optimizing a Trainium2 kernel written in Bass

```python
import math
import jax, jax.numpy as jnp
from jax import lax
import numpy as np

D_MODEL = 1024
BATCH = 2
SEQ = 8192
DEPTH = 4

CHUNK = 64
N_META = 16
D_MIX = D_MODEL
D_POOL = D_MIX // 2
D_RNN = D_MIX // 2
POOL_WINDOWS = (2, 4, 8, 16)
N_POOL_GROUPS = len(POOL_WINDOWS)
POOL_GROUP_DIM = D_POOL // N_POOL_GROUPS
N_RNN_HEADS = 8
RNN_HEAD_DIM = D_RNN // N_RNN_HEADS
CONV_WIDTH = 4
LRU_C = 8.0
D_IN_PROJ = D_POOL + D_RNN + D_RNN
D_FF = 4 * D_MODEL
EPS = 1e-6

kernel_name = "hybrid_pool_rglru_encoder"


def rms_norm(x, g):
    xf = x.astype(jnp.float32)
    y = xf * lax.rsqrt(jnp.mean(xf * xf, axis=-1, keepdims=True) + EPS)
    return (y * g.astype(jnp.float32)).astype(x.dtype)


def multiscale_pool_mixer(u, pool_w, pool_b, pool_scale):
    B, T, _ = u.shape
    uf = u.astype(jnp.float32)
    cs = jnp.concatenate([jnp.zeros((B, 1, D_POOL), jnp.float32), jnp.cumsum(uf, axis=1)], axis=1)
    upper = cs[:, 1:]
    t_idx = jnp.arange(T, dtype=jnp.float32)[None, :, None]
    pooled = []
    for g, k in enumerate(POOL_WINDOWS):
        sl = slice(g * POOL_GROUP_DIM, (g + 1) * POOL_GROUP_DIM)
        cs_g = cs[:, :, sl]
        lower = jnp.pad(cs_g[:, :T + 1 - k], ((0, 0), (k - 1, 0), (0, 0)))
        count = jnp.minimum(t_idx + 1.0, float(k))
        pooled.append((upper[:, :, sl] - lower) / count)
    pooled = jnp.concatenate(pooled, axis=-1) - uf
    pg = pooled.astype(u.dtype).reshape(B, T, N_POOL_GROUPS, POOL_GROUP_DIM)
    mapped = jnp.einsum('btgi,gij->btgj', pg, pool_w).reshape(B, T, D_POOL) + pool_b
    return mapped * pool_scale


def rglru_mixer(u, gate, conv_w, conv_b, gate_r_w, gate_r_b, gate_i_w, gate_i_b, lru_lambda):
    B, T, _ = u.shape
    upad = jnp.pad(u, ((0, 0), (CONV_WIDTH - 1, 0), (0, 0)))
    xc = conv_b + sum(upad[:, k:k + T] * conv_w[k] for k in range(CONV_WIDTH))
    xh = xc.reshape(B, T, N_RNN_HEADS, RNN_HEAD_DIM)
    r = jax.nn.sigmoid((jnp.einsum('bthi,hij->bthj', xh, gate_r_w).reshape(B, T, D_RNN) + gate_r_b).astype(jnp.float32))
    i = jax.nn.sigmoid((jnp.einsum('bthi,hij->bthj', xh, gate_i_w).reshape(B, T, D_RNN) + gate_i_b).astype(jnp.float32))
    log_a = -LRU_C * r * jax.nn.softplus(-lru_lambda.astype(jnp.float32))
    a = jnp.exp(log_a)
    mult = jnp.sqrt(-jnp.expm1(2.0 * log_a))
    b = mult * (i * xc.astype(jnp.float32))

    def combine(left, right):
        a_l, b_l = left
        a_r, b_r = right
        return a_l * a_r, a_r * b_l + b_r

    _, h = lax.associative_scan(combine, (a, b), axis=1)
    return h.astype(u.dtype) * jax.nn.gelu(gate)


def hybrid_mixer(xn, w_in, pool_w, pool_b, pool_scale, conv_w, conv_b, gate_r_w, gate_r_b,
                 gate_i_w, gate_i_b, lru_lambda, group_norm_g, w_out):
    proj = xn @ w_in
    u_pool = proj[..., :D_POOL]
    u_rnn = proj[..., D_POOL:D_POOL + D_RNN]
    u_gate = proj[..., D_POOL + D_RNN:]
    y_pool = multiscale_pool_mixer(u_pool, pool_w, pool_b, pool_scale)
    y_rnn = rglru_mixer(u_rnn, u_gate, conv_w, conv_b, gate_r_w, gate_r_b, gate_i_w, gate_i_b, lru_lambda)
    y = jnp.concatenate([rms_norm(y_pool, group_norm_g[:D_POOL]),
                         rms_norm(y_rnn, group_norm_g[D_POOL:])], axis=-1)
    return y @ w_out


def sq_relu_mlp(xn, w_up, w_down):
    h = jax.nn.relu(xn @ w_up)
    return (h * h) @ w_down


def setup_inputs(seed: int = 0) -> dict:
    key = jax.random.key(seed)
    ks = jax.random.split(key, 24)
    f32 = jnp.float32
    nrm = lambda k, shape, s: jax.random.normal(k, shape, f32) * s
    a0 = jax.random.uniform(ks[11], (DEPTH, D_RNN), f32, 0.9, 0.999)
    p = a0 ** (1.0 / LRU_C)
    lru_lambda = jnp.log(p) - jnp.log1p(-p)
    return {
        "x": nrm(ks[0], (BATCH, SEQ, D_MODEL), 1.0),
        "meta_tokens": nrm(ks[1], (N_META, D_MODEL), 1.0),
        "mix_norm_g": 1.0 + nrm(ks[2], (DEPTH, D_MODEL), 0.1),
        "w_in": nrm(ks[3], (DEPTH, D_MODEL, D_IN_PROJ), D_MODEL ** -0.5),
        "pool_w": nrm(ks[4], (DEPTH, N_POOL_GROUPS, POOL_GROUP_DIM, POOL_GROUP_DIM), POOL_GROUP_DIM ** -0.5),
        "pool_b": nrm(ks[5], (DEPTH, D_POOL), 0.02),
        "pool_scale": 0.5 + nrm(ks[6], (DEPTH, D_POOL), 0.1),
        "conv_w": nrm(ks[7], (DEPTH, CONV_WIDTH, D_RNN), CONV_WIDTH ** -0.5),
        "conv_b": nrm(ks[8], (DEPTH, D_RNN), 0.02),
        "gate_r_w": nrm(ks[9], (DEPTH, N_RNN_HEADS, RNN_HEAD_DIM, RNN_HEAD_DIM), RNN_HEAD_DIM ** -0.5),
        "gate_r_b": nrm(ks[10], (DEPTH, D_RNN), 0.02),
        "gate_i_w": nrm(ks[12], (DEPTH, N_RNN_HEADS, RNN_HEAD_DIM, RNN_HEAD_DIM), RNN_HEAD_DIM ** -0.5),
        "gate_i_b": nrm(ks[13], (DEPTH, D_RNN), 0.02),
        "lru_lambda": lru_lambda,
        "group_norm_g": 1.0 + nrm(ks[14], (DEPTH, D_MIX), 0.1),
        "w_out": nrm(ks[15], (DEPTH, D_MIX, D_MODEL), D_MIX ** -0.5),
        "mlp_norm_g": 1.0 + nrm(ks[16], (DEPTH, D_MODEL), 0.1),
        "w_up": nrm(ks[17], (DEPTH, D_MODEL, D_FF), D_MODEL ** -0.5),
        "w_down": nrm(ks[18], (DEPTH, D_FF, D_MODEL), D_FF ** -0.5),
        "final_norm_g": 1.0 + nrm(ks[19], (D_MODEL,), 0.1),
    }


def reference(x, meta_tokens, mix_norm_g, w_in, pool_w, pool_b, pool_scale, conv_w, conv_b,
              gate_r_w, gate_r_b, gate_i_w, gate_i_b, lru_lambda, group_norm_g, w_out,
              mlp_norm_g, w_up, w_down, final_norm_g):
    B = x.shape[0]
    meta = jnp.broadcast_to(meta_tokens.astype(x.dtype)[None], (B, N_META, D_MODEL))
    h = jnp.concatenate([meta, x], axis=1)
    for l in range(DEPTH):
        h = h + hybrid_mixer(rms_norm(h, mix_norm_g[l]), w_in[l], pool_w[l], pool_b[l], pool_scale[l],
                             conv_w[l], conv_b[l], gate_r_w[l], gate_r_b[l], gate_i_w[l], gate_i_b[l],
                             lru_lambda[l], group_norm_g[l], w_out[l])
        h = h + sq_relu_mlp(rms_norm(h, mlp_norm_g[l]), w_up[l], w_down[l])
    h = rms_norm(h, final_norm_g)
    return h[:, N_META:]
```

```python
import numpy as np
from contextlib import ExitStack
import concourse.bass as bass
import concourse.mybir as mybir
from concourse.bass_utils import run_bass_kernel_spmd

F32 = mybir.dt.float32
BF16 = mybir.dt.bfloat16
ALU = mybir.AluOpType
AF = mybir.ActivationFunctionType

D = 1024
DFF = 4096
NPRE = 16
NMAIN = 2048
NT = NPRE + NMAIN
NLAYERS = 4
EPS = 1e-6
PL = 64
P_FIN = NLAYERS * PL
P_SEL = P_FIN + 8
P_MPREV = P_SEL + 4
P_M0 = P_MPREV + 4
P_INVC = P_M0 + 4
NP = P_INVC + 64
TILES = [(0, NPRE)] + [(NPRE + 512 * i, 512) for i in range(4)]
GELU_C1 = 0.7978845608028654
GELU_C2 = 0.044715


def I(name, *a, **kw):
    return lambda e: getattr(e, name)(*a, **kw)


class Res:
    __slots__ = ("name", "w", "rs")

    def __init__(self, name):
        self.name = name
        self.w = None
        self.rs = []


class Prog:
    def __init__(self, nc, stack):
        self.nc = nc
        self.stack = stack
        self.eng = {}
        for name in ("vector", "scalar", "gpsimd", "tensor", "sync"):
            sem = stack.enter_context(nc.semaphore("e_" + name))
            self.eng[name] = dict(sem=sem, cnt=0, seen={}, ops=[])
        self.dsem = {}

    def new_dsem(self, name):
        sem = self.stack.enter_context(self.nc.semaphore("d_" + name))
        self.dsem[name] = dict(sem=sem, cnt=0)
        return name

    def _waits(self, ename, reads, writes):
        e = self.eng[ename]
        deps = []
        for r in reads:
            if r.w is not None:
                deps.append(r.w + ("RAW",))
        for w in writes:
            if w.w is not None:
                deps.append(w.w + ("WAW",))
            for s in w.rs:
                deps.append(s + ("WAR",))
        waits = []
        for (skey, sem, val, src, kind) in deps:
            if src == ename and ename == "tensor":
                continue
            if e["seen"].get(skey, 0) >= val:
                continue
            e["seen"][skey] = val
            waits.append((sem, val))
        return waits

    def _stamp(self, stamp, reads, writes):
        for w in writes:
            w.w = stamp
            w.rs = []
        for r in reads:
            r.rs.append(stamp)

    def op(self, ename, fn, reads=(), writes=()):
        self.group(ename, [fn], reads, writes)

    def group(self, ename, fns, reads=(), writes=()):
        e = self.eng[ename]
        waits = self._waits(ename, reads, writes)
        e["cnt"] += 1
        for i, fn in enumerate(fns):
            e["ops"].append((waits if i == 0 else [], fn, (e["sem"], 1) if i == len(fns) - 1 else None))
        self._stamp(("e_" + ename, e["sem"], e["cnt"], ename), reads, writes)

    def dma(self, qname, fn, reads, writes, dname, amt=16):
        e = self.eng[qname]
        d = self.dsem[dname]
        fns = fn if isinstance(fn, list) else [fn]
        waits = self._waits(qname, reads, writes)
        for i, f in enumerate(fns):
            d["cnt"] += amt
            e["ops"].append((waits if i == 0 else [], f, (d["sem"], amt)))
        self._stamp(("d_" + dname, d["sem"], d["cnt"], None), reads, writes)

    def wait_all(self, qname, resources):
        e = self.eng[qname]
        waits = self._waits(qname, resources, ())
        if waits:
            e["ops"].append((waits, None, None))

    def emit(self, block):
        def run(ename):
            def body(eng):
                for (waits, fn, inc) in self.eng[ename]["ops"]:
                    for (sem, val) in waits:
                        eng.wait_ge(sem, val)
                    if fn is None:
                        continue
                    ins = fn(eng)
                    if inc is not None:
                        ins.then_inc(inc[0], inc[1])
            return body
        block.sync(run("sync"))
        block.gpsimd(run("gpsimd"))
        block.tensor(run("tensor"))
        block.scalar(run("scalar"))
        block.vector(run("vector"))


def build_nc(nlayers=NLAYERS):
    nc = bass.Bass("TRN2", target_bir_lowering=False)
    dt_in = lambda name, shape: nc.dram_tensor(name, shape, F32, kind="ExternalInput").ap()
    xT = dt_in("xT", [D, NT])
    prm_d = dt_in("prm", [128, NP])
    w_in_d = dt_in("w_in", [NLAYERS, D, 1536])
    pool_w_d = dt_in("pool_w", [NLAYERS, 4, 128, 128])
    gr_d = dt_in("gate_r_w", [NLAYERS, 8, 64, 64])
    gi_d = dt_in("gate_i_w", [NLAYERS, 8, 64, 64])
    w_out_d = dt_in("w_out", [NLAYERS, D, D])
    w_up_d = dt_in("w_up", [NLAYERS, D, DFF])
    w_down_d = dt_in("w_down", [NLAYERS, DFF, D])
    outT = nc.dram_tensor("outT", [D, NMAIN], F32, kind="ExternalOutput").ap()
    halo_src = [nc.dram_tensor(f"halo_src{l}", [128, 128], F32) for l in range(nlayers)]
    halo_dst = [nc.dram_tensor(f"halo_dst{l}", [512, 128], F32) for l in range(nlayers)]
    sum_src = [nc.dram_tensor(f"sum_src{l}", [128, 8], F32) for l in range(nlayers)]
    sum_dst = [nc.dram_tensor(f"sum_dst{l}", [512, 8], F32) for l in range(nlayers)]
    groups = [[0, 1, 2, 3], [4, 5, 6, 7]]

    with ExitStack() as st:
        def sb(name, shape, dt):
            return st.enter_context(nc.sbuf_tensor(name, shape, dt))

        P = Prog(nc, st)
        H = sb("H", [128, 8, NT], F32)
        HG = sb("HG", [128, 4, NT], F32)
        XNv = HG[:].bitcast(BF16)
        PG = sb("PG", [128, 4, NMAIN], BF16)
        YPn = sb("YPn", [128, 4, NT], BF16)
        W1 = sb("W1", [128, 8, 1536], BF16)
        RING = [sb(f"ring{i}", [128, 8192], BF16) for i in range(2)]
        RINGF = [r[:].bitcast(F32) for r in RING]
        S = [RINGF[j // 7][:, (j % 7) * 528:(j % 7 + 1) * 528] for j in range(14)]
        prm = sb("prm_sb", [128, NP], F32)
        prd = sb("prd", [128, NLAYERS * 48 + 8], F32)
        ones = sb("ones", [128, 128], BF16)
        pw = sb("pw", [128, 4, 128], BF16)
        grw = sb("grw", [128, 4, 128], BF16)
        giw = sb("giw", [128, 4, 128], BF16)
        sq = sb("sq", [128, 2, 512], BF16)
        xb = sb("xb", [128, 8, 512], BF16)
        hid = [xb[:, 0:4, :], xb[:, 4:8, :]]
        rstd = sb("rstd", [128, 512], F32)
        pooled = sb("pooled", [128, 2, 512], BF16)
        xcb = sb("xcb", [128, 512], BF16)
        UPh = sb("UPh", [128, 4, 16], F32)
        URh = sb("URh", [128, 4, 16], F32)
        state = sb("state", [128, 8], F32)
        hsrc = sb("hsrc", [128, 8, 16], F32)
        hb = sb("hb", [128, 8, 16], F32)
        srecv = sb("srecv", [128, 4, 8], F32)
        carry = sb("carry", [128, 4], F32)
        ctmp = sb("ctmp", [128, 4], F32)
        expc = sb("expc", [128, 2], F32)
        ps = [st.enter_context(nc.psum_tensor(f"ps{i}", [128, 512], F32)) for i in range(8)]

        R = {}

        def res(name):
            if name not in R:
                R[name] = Res(name)
            return R[name]

        def transfer(dsts, srcs):
            for d in dsts:
                for s in srcs:
                    if s.w is not None:
                        d.rs.append(s.w)
                    d.rs.extend(s.rs)

        rH = lambda c, t: res(f"H{c}_{t}")
        rS = [res(f"S{j}") for j in range(14)]
        rring = [res("ring0"), res("ring1")]
        rxb = [res(f"xb{c}") for c in range(8)]
        rhid = [res("hid0"), res("hid1")]
        rps = [res(f"ps{i}") for i in range(8)]
        bank_i = [0]

        def bank():
            i = bank_i[0] % 8
            bank_i[0] += 1
            return ps[i], rps[i]

        for n_ in (["ld_prm", "w1", "pw", "grw", "giw", "ring0", "ring1", "halo_o", "halo_cc", "halo_i",
                    "sum_o", "sum_cc", "sum_i"] + [f"ld_x{i}" for i in range(5)] + [f"out{i}" for i in range(8)]):
            P.new_dsem(n_)

        P.dma("sync", I("dma_start", out=prm[:], in_=prm_d), [], [res("prm")], "ld_prm")
        for ti, (c0, n) in enumerate(TILES):
            P.dma("sync", I("dma_start",
                out=H[:, :, c0:c0 + n], in_=xT.rearrange("(c p) t -> p c t", p=128)[:, :, c0:c0 + n]),
                [], [rH(c, ti) for c in range(8)], f"ld_x{ti}")
        P.op("vector", I("memset", ones[:], 1.0), [], [res("ones")])
        P.op("vector", I("memset", expc[:, 0:1], -0.5), [], [res("expc")])
        P.op("vector", I("memset", expc[:, 1:2], 0.5), [], [res("expc")])
        P.op("vector", I("memset", grw[:], 0.0), [], [res("grw")])
        P.op("vector", I("memset", giw[:], 0.0), [], [res("giw")])

        rprd = res("prd")
        rprm = res("prm")
        DB = lambda l: l * 48
        for l in range(NLAYERS):
            b0, d0 = l * PL, DB(l)
            P.op("vector", I("tensor_scalar",
                out=prd[:, d0:d0 + 16], in0=prm[:, b0:b0 + 16], scalar1=32.0, scalar2=None, op0=ALU.mult),
                [rprm], [rprd])
            P.op("vector", I("tensor_scalar",
                out=prd[:, d0 + 16:d0 + 24], in0=prm[:, b0 + 16:b0 + 24], scalar1=float(np.sqrt(512.0)),
                scalar2=None, op0=ALU.mult), [rprm], [rprd])
            P.op("vector", I("tensor_scalar",
                out=prd[:, d0 + 24:d0 + 32], in0=prm[:, b0 + 52:b0 + 60], scalar1=0.5, scalar2=None,
                op0=ALU.mult), [rprm], [rprd])
            P.op("scalar", I("activation",
                out=prd[:, d0 + 40:d0 + 44], in_=prm[:, b0 + 60:b0 + 64], func=AF.Exp, scale=-1.0),
                [rprm], [rprd])
        for l in range(NLAYERS):
            d0 = DB(l)
            P.op("scalar", I("activation",
                out=prd[:, d0 + 44:d0 + 48], in_=prd[:, d0 + 40:d0 + 44], func=AF.Ln, bias=1.0, scale=1.0),
                [rprd], [rprd])
            P.op("vector", I("tensor_scalar",
                out=prd[:, d0 + 32:d0 + 36], in0=prd[:, d0 + 44:d0 + 48], scalar1=-8.0, scalar2=None,
                op0=ALU.mult), [rprd], [rprd])
            P.op("vector", I("tensor_scalar",
                out=prd[:, d0 + 36:d0 + 40], in0=prd[:, d0 + 44:d0 + 48], scalar1=-4.0, scalar2=None,
                op0=ALU.mult), [rprd], [rprd])
        DFIN = NLAYERS * 48
        P.op("vector", I("tensor_scalar",
            out=prd[:, DFIN:DFIN + 8], in0=prm[:, P_FIN:P_FIN + 8], scalar1=32.0, scalar2=None, op0=ALU.mult),
            [rprm], [rprd])

        pcol = lambda j: prm[:, j:j + 1]
        dcol = lambda j: prd[:, j:j + 1]
        sq_i = [0]

        def rms_stats(src_fn, nchunks, n, src_res, add_const):
            pt, pr = bank()
            for c in range(nchunks):
                j = sq_i[0] % 2
                sq_i[0] += 1
                P.op("scalar", I("activation", out=sq[:, j, :n], in_=src_fn(c), func=AF.Square),
                     [src_res(c)], [res(f"sq{j}")])
                P.op("tensor", I("matmul", out=pt[:, :n], lhsT=ones[:], rhs=sq[:, j, :n],
                                                           start=(c == 0), stop=(c == nchunks - 1)),
                     [res("ones"), res(f"sq{j}")], [pr])
            P.op("vector", I("tensor_scalar", out=rstd[:, :n], in0=pt[:, :n], scalar1=float(add_const),
                             scalar2=None, op0=ALU.add), [pr], [res("rstd")])
            P.op("gpsimd", I("tensor_tensor", out=rstd[:, :n], in0=rstd[:, :n],
                             in1=expc[:, 0:1].to_broadcast([128, n]), op=ALU.pow),
                 [res("rstd"), res("expc")], [res("rstd")])

        def load_w1(src3, ncols):
            P.dma("gpsimd", [I("dma_start", out=W1[:, k:k + 2, :ncols], in_=src3[:, k:k + 2, :])
                             for k in range(0, 8, 2)], [], [res("W1")], "w1")

        def load_small(l):
            P.dma("gpsimd", I("dma_start", out=pw[:], in_=pool_w_d[l].rearrange("g i j -> i g j")),
                  [], [res("pw")], "pw")
            for (dst, srcd, rn) in ((grw, gr_d, "grw"), (giw, gi_d, "giw")):
                s4 = srcd[l].rearrange("(q two) i j -> two i q j", two=2)
                P.dma("gpsimd", [I("dma_start", out=dst[64 * b:64 * b + 64, :, 64 * b:64 * b + 64], in_=s4[b])
                                 for b in range(2)], [], [res(rn)], rn)

        def inproj_norm(c0, n, ti, gbase):
            rms_stats(lambda c: H[:, c, c0:c0 + n], 8, n, lambda c: rH(c, ti), D * EPS)
            for c in range(8):
                P.op("vector", I("scalar_tensor_tensor",
                    out=xb[:, c, :n], in0=H[:, c, c0:c0 + n], scalar=dcol(gbase + c), in1=rstd[:, :n],
                    op0=ALU.mult, op1=ALU.mult), [rH(c, ti), res("rstd"), rprd], [rxb[c]])

        def inproj_mm(m, n):
            pt, pr = bank()
            P.group("tensor", [
                (I("matmul", out=pt[:, :n], lhsT=W1[:, kc, m * 128:(m + 1) * 128], rhs=xb[:, kc, :n],
                                           start=(kc == 0), stop=(kc == 7)))
                for kc in range(8)], [res("W1")] + rxb, [pr])
            return pt, pr

        rHG = lambda m, ti: res(f"HG{m}_{ti}")
        rXN = lambda c, ti: res(f"XN{c}_{ti}")
        allHG = [rHG(m, ti) for m in range(4) for ti in range(5)]
        allXN = [rXN(c, ti) for c in range(8) for ti in range(5)]
        XN = lambda c, c0, n: XNv[:, c // 2, (c % 2) * NT + c0:(c % 2) * NT + c0 + n]

        load_w1(w_in_d[0].rearrange("(k p) n -> p k n", p=128), 1536)
        for l in range(nlayers):
            b0, d0 = l * PL, DB(l)
            load_small(l)
            if l > 0:
                for i in range(2):
                    transfer(rS[7 * i:7 * i + 7], [rring[i]])
                transfer(rxb[0:4], [rhid[0]])
                transfer(rxb[4:8], [rhid[1]])
                transfer(allHG, allXN)
            P.op("vector", I("memset", state[:, 0:4], 0.0), [], [res("state")])
            P.op("vector", I("memset", state[:, 4:8], 1.0), [], [res("state")])
            P.op("vector", I("memset", UPh[:], 0.0), [], [res("UPh")])
            P.op("vector", I("memset", URh[:], 0.0), [], [res("URh")])

            c0t = NT - 16
            inproj_norm(c0t, 16, 4, d0 + 0)
            for m in range(8):
                pt, pr = inproj_mm(m, 16)
                P.op("scalar", I("copy", out=hsrc[:, m, :], in_=pt[:, :16]), [pr], [res("hsrc")])
            P.dma("sync", I("dma_start", out=halo_src[l].ap(), in_=hsrc[:].rearrange("p m j -> p (m j)")),
                  [res("hsrc")], [res(f"halo_src{l}")], "halo_o")
            P.dma("gpsimd", I("collective_compute",
                "AllGather", ALU.bypass, replica_groups=groups, ins=[halo_src[l].ap()], outs=[halo_dst[l].ap()]),
                [res(f"halo_src{l}")], [res(f"halo_dst{l}")], "halo_cc", amt=1)
            hrecv = S[6][:, 0:512].rearrange("p (r f) -> p r f", r=4)
            P.dma("sync", I("dma_start",
                out=hrecv, in_=halo_dst[l].ap().rearrange("(r p) f -> p r f", p=128)),
                [res(f"halo_dst{l}")], [rS[6]], "halo_i")
            hbf = hb[:].rearrange("p m j -> p (m j)")
            P.op("vector", I("tensor_scalar", out=hbf, in0=hrecv[:, 0, :], scalar1=pcol(P_SEL), scalar2=None,
                                                     op0=ALU.mult), [rS[6], rprm], [res("hb")])
            for r in range(1, 4):
                P.op("vector", I("scalar_tensor_tensor",
                    out=hbf, in0=hrecv[:, r, :], scalar=pcol(P_SEL + r), in1=hbf, op0=ALU.mult, op1=ALU.add),
                    [rS[6], res("hb"), rprm], [res("hb")])

            for ti, (c0, n) in enumerate(TILES):
                is_pre = (ti == 0)
                W = 16 + n
                inproj_norm(c0, n, ti, d0 + 0)
                for m in range(4):
                    k = 2 << m
                    pt, pr = inproj_mm(m, n)
                    P.op("vector", I("tensor_copy", out=S[0][:, 0:16], in_=UPh[:, m, :]),
                         [res("UPh")], [rS[0]])
                    P.op("scalar", I("copy", out=S[0][:, 16:W], in_=pt[:, :n]), [pr], [rS[0]])
                    if is_pre:
                        P.op("vector", I("scalar_tensor_tensor",
                            out=S[0][:, 16:32], in0=S[0][:, 16:32], scalar=pcol(P_M0), in1=hb[:, m, :],
                            op0=ALU.mult, op1=ALU.add), [rS[0], res("hb"), rprm], [rS[0]])
                    P.op("vector", I("tensor_copy", out=UPh[:, m, :], in_=S[0][:, n:W]),
                         [rS[0]], [res("UPh")])
                    cur, curres = 0, rS[0]
                    step = 1
                    bi = 1
                    while step < k:
                        lo = 2 * step - 1
                        P.op("vector", I("tensor_tensor",
                            out=S[bi][:, lo:W], in0=S[cur][:, lo:W], in1=S[cur][:, lo - step:W - step], op=ALU.add),
                            [curres], [rS[bi]])
                        cur, curres = bi, rS[bi]
                        step *= 2
                        bi = 3 - bi
                    pj = m % 2
                    if is_pre:
                        o = 3 - cur
                        P.op("vector", I("tensor_tensor",
                            out=S[o][:, 0:16], in0=S[cur][:, 16:32],
                            in1=prm[:, P_INVC + 16 * m:P_INVC + 16 * m + 16], op=ALU.mult),
                            [curres, rprm], [rS[o]])
                        P.op("vector", I("tensor_tensor",
                            out=pooled[:, pj, :16], in0=S[o][:, 0:16], in1=S[0][:, 16:32], op=ALU.subtract),
                            [rS[o], rS[0]], [res(f"pooled{pj}")])
                    else:
                        P.op("vector", I("scalar_tensor_tensor",
                            out=pooled[:, pj, :n], in0=S[cur][:, 16:W], scalar=1.0 / k, in1=S[0][:, 16:W],
                            op0=ALU.mult, op1=ALU.subtract), [curres, rS[0]], [res(f"pooled{pj}")])
                    pt2, pr2 = bank()
                    P.op("tensor", I("matmul",
                        out=pt2[:, :n], lhsT=pw[:, m, :], rhs=pooled[:, pj, :n], start=True, stop=True),
                        [res("pw"), res(f"pooled{pj}")], [pr2])
                    P.op("vector", I("tensor_scalar",
                        out=S[3 + m][:, :n], in0=pt2[:, :n], scalar1=pcol(b0 + 24 + m), scalar2=pcol(b0 + 28 + m),
                        op0=ALU.add, op1=ALU.mult), [pr2, rprm], [rS[3 + m]])
                rms_stats(lambda c: S[3 + c][:, :n], 4, n, lambda c: rS[3 + c], 512 * EPS)
                for m in range(4):
                    P.op("vector", I("scalar_tensor_tensor",
                        out=YPn[:, m, c0:c0 + n], in0=S[3 + m][:, :n], scalar=dcol(d0 + 16 + m), in1=rstd[:, :n],
                        op0=ALU.mult, op1=ALU.mult), [rS[3 + m], res("rstd"), rprd], [res(f"YPn{m}_{ti}")])

                for m in range(4):
                    pt, pr = inproj_mm(4 + m, n)
                    P.op("vector", I("tensor_copy", out=S[7][:, 0:16], in_=URh[:, m, :]),
                         [res("URh")], [rS[7]])
                    P.op("scalar", I("copy", out=S[7][:, 16:W], in_=pt[:, :n]), [pr], [rS[7]])
                    if is_pre:
                        P.op("vector", I("scalar_tensor_tensor",
                            out=S[7][:, 16:32], in0=S[7][:, 16:32], scalar=pcol(P_M0), in1=hb[:, 4 + m, :],
                            op0=ALU.mult, op1=ALU.add), [rS[7], res("hb"), rprm], [rS[7]])
                    P.op("vector", I("tensor_copy", out=URh[:, m, :], in_=S[7][:, n:W]),
                         [rS[7]], [res("URh")])
                    ptg, prg = inproj_mm(8 + m, n)
                    P.op("vector", I("tensor_copy", out=S[9][:, :n], in_=ptg[:, :n]), [prg], [rS[9]])
                    P.op("scalar", I("activation", out=S[10][:, :n], in_=ptg[:, :n], func=AF.Square),
                         [prg], [rS[10]])
                    P.op("vector", I("tensor_scalar",
                        out=S[8][:, :n], in0=S[7][:, 13:13 + n], scalar1=pcol(b0 + 32 + m), scalar2=pcol(b0 + 48 + m),
                        op0=ALU.mult, op1=ALU.add), [rS[7], rprm], [rS[8]])
                    for k in range(1, 4):
                        P.op("vector", I("scalar_tensor_tensor",
                            out=S[8][:, :n], in0=S[7][:, 13 + k:13 + k + n], scalar=pcol(b0 + 32 + 4 * k + m),
                            in1=S[8][:, :n], op0=ALU.mult, op1=ALU.add), [rS[7], rS[8], rprm], [rS[8]])
                    P.op("scalar", I("copy", out=xcb[:, :n], in_=S[8][:, :n]), [rS[8]], [res("xcb")])
                    ptr, prr = bank()
                    P.op("tensor", I("matmul", out=ptr[:, :n], lhsT=grw[:, m, :],
                                                                   rhs=xcb[:, :n], start=True, stop=True),
                         [res("grw"), res("xcb")], [prr])
                    pti, pri = bank()
                    P.op("tensor", I("matmul", out=pti[:, :n], lhsT=giw[:, m, :],
                                                                   rhs=xcb[:, :n], start=True, stop=True),
                         [res("giw"), res("xcb")], [pri])
                    P.op("scalar", I("activation",
                        out=S[11][:, :n], in_=ptr[:, :n], func=AF.Tanh, bias=dcol(d0 + 24 + m), scale=0.5),
                        [prr, rprd], [rS[11]])
                    P.op("scalar", I("activation",
                        out=S[12][:, :n], in_=pti[:, :n], func=AF.Tanh, bias=dcol(d0 + 28 + m), scale=0.5),
                        [pri, rprd], [rS[12]])
                    P.op("scalar", I("activation",
                        out=S[1][:, :n], in_=S[11][:, :n], func=AF.Exp, bias=dcol(d0 + 36 + m), scale=dcol(d0 + 36 + m)),
                        [rS[11], rprd], [rS[1]])
                    P.op("scalar", I("activation",
                        out=S[13][:, :n], in_=S[11][:, :n], func=AF.Exp, bias=dcol(d0 + 32 + m), scale=dcol(d0 + 32 + m)),
                        [rS[11], rprd], [rS[13]])
                    P.op("vector", I("tensor_scalar", out=S[13][:, :n], in0=S[13][:, :n], scalar1=-0.25,
                                                             scalar2=0.25, op0=ALU.mult, op1=ALU.add),
                         [rS[13]], [rS[13]])
                    P.op("vector", I("scalar_tensor_tensor",
                        out=S[12][:, :n], in0=S[12][:, :n], scalar=1.0, in1=S[8][:, :n], op0=ALU.add, op1=ALU.mult),
                        [rS[12], rS[8]], [rS[12]])
                    P.op("gpsimd", I("tensor_tensor", out=S[13][:, :n], in0=S[13][:, :n],
                                     in1=expc[:, 1:2].to_broadcast([128, n]), op=ALU.pow),
                         [rS[13], res("expc")], [rS[13]])
                    P.op("vector", I("tensor_tensor", out=S[12][:, :n], in0=S[13][:, :n], in1=S[12][:, :n],
                                     op=ALU.mult), [rS[13], rS[12]], [rS[12]])
                    P.op("vector", I("tensor_tensor_scan",
                        out=S[2][:, :n], data0=S[1][:, :n], data1=S[12][:, :n], initial=state[:, m:m + 1],
                        op0=ALU.mult, op1=ALU.add), [rS[1], rS[12], res("state")], [rS[2]])
                    if is_pre:
                        P.op("vector", I("tensor_scalar",
                            out=state[:, m:m + 1], in0=S[2][:, n - 1:n], scalar1=pcol(P_M0), scalar2=None, op0=ALU.mult),
                            [rS[2], rprm], [res("state")])
                    else:
                        P.op("vector", I("tensor_copy", out=state[:, m:m + 1], in_=S[2][:, n - 1:n]),
                             [rS[2]], [res("state")])
                        P.op("vector", I("tensor_tensor_scan",
                            out=S[0][:, :n], data0=S[1][:, :n], data1=S[1][:, :n], initial=state[:, 4 + m:5 + m],
                            op0=ALU.mult, op1=ALU.min), [rS[1], res("state")], [rS[0]])
                        P.op("vector", I("tensor_copy", out=state[:, 4 + m:5 + m], in_=S[0][:, n - 1:n]),
                             [rS[0]], [res("state")])
                    P.op("vector", I("tensor_scalar",
                        out=S[10][:, :n], in0=S[10][:, :n], scalar1=GELU_C2, scalar2=1.0, op0=ALU.mult, op1=ALU.add),
                        [rS[10]], [rS[10]])
                    P.op("vector", I("tensor_tensor",
                        out=S[10][:, :n], in0=S[10][:, :n], in1=S[9][:, :n], op=ALU.mult),
                        [rS[10], rS[9]], [rS[10]])
                    P.op("scalar", I("activation", out=S[10][:, :n], in_=S[10][:, :n], func=AF.Tanh,
                                                          scale=GELU_C1), [rS[10]], [rS[10]])
                    P.op("vector", I("scalar_tensor_tensor",
                        out=S[10][:, :n], in0=S[10][:, :n], scalar=1.0, in1=S[9][:, :n], op0=ALU.add, op1=ALU.mult),
                        [rS[10], rS[9]], [rS[10]])
                    P.op("vector", I("scalar_tensor_tensor",
                        out=HG[:, m, c0:c0 + n], in0=S[2][:, :n], scalar=0.5, in1=S[10][:, :n],
                        op0=ALU.mult, op1=ALU.mult), [rS[2], rS[10]], [rHG(m, ti)])
                    if not is_pre:
                        P.op("vector", I("scalar_tensor_tensor",
                            out=PG[:, m, c0 - 16:c0 - 16 + n], in0=S[0][:, :n], scalar=0.5, in1=S[10][:, :n],
                            op0=ALU.mult, op1=ALU.mult), [rS[0], rS[10]], [res(f"PG{m}_{ti}")])

            P.dma("sync", I("dma_start", out=sum_src[l].ap(), in_=state[:]),
                  [res("state")], [res(f"sum_src{l}")], "sum_o")
            P.dma("gpsimd", I("collective_compute",
                "AllGather", ALU.bypass, replica_groups=groups, ins=[sum_src[l].ap()], outs=[sum_dst[l].ap()]),
                [res(f"sum_src{l}")], [res(f"sum_dst{l}")], "sum_cc", amt=1)
            P.dma("sync", I("dma_start",
                out=srecv[:], in_=sum_dst[l].ap().rearrange("(r p) f -> p r f", p=128)),
                [res(f"sum_dst{l}")], [res("srecv")], "sum_i")
            load_w1(w_out_d[l].rearrange("(k p) n -> p k n", p=128), 1024)
            P.op("vector", I("memset", carry[:], 0.0), [], [res("carry")])
            for r in range(3):
                P.op("vector", I("tensor_tensor", out=ctmp[:], in0=srecv[:, r, 4:8], in1=carry[:],
                                                            op=ALU.mult), [res("srecv"), res("carry")], [res("ctmp")])
                P.op("vector", I("tensor_tensor", out=ctmp[:], in0=ctmp[:], in1=srecv[:, r, 0:4],
                                                            op=ALU.add), [res("srecv"), res("ctmp")], [res("ctmp")])
                P.op("vector", I("tensor_tensor", out=ctmp[:], in0=ctmp[:], in1=carry[:],
                                                            op=ALU.subtract), [res("carry"), res("ctmp")], [res("ctmp")])
                P.op("vector", I("scalar_tensor_tensor",
                    out=carry[:], in0=ctmp[:], scalar=pcol(P_MPREV + r), in1=carry[:], op0=ALU.mult, op1=ALU.add),
                    [res("ctmp"), res("carry"), rprm], [res("carry")])

            ynv = [S[4].bitcast(BF16), S[5].bitcast(BF16)]
            YN = lambda m, n: ynv[m // 2][:, (m % 2) * 512:(m % 2) * 512 + n]
            rYN = lambda m: rS[4 + m // 2]
            for ti, (c0, n) in enumerate(TILES):
                is_pre = (ti == 0)
                for m in range(4):
                    if is_pre:
                        P.op("vector", I("tensor_copy", out=S[m][:, :n], in_=HG[:, m, c0:c0 + n]),
                             [rHG(m, ti)], [rS[m]])
                    else:
                        P.op("vector", I("scalar_tensor_tensor",
                            out=S[m][:, :n], in0=PG[:, m, c0 - 16:c0 - 16 + n], scalar=carry[:, m:m + 1],
                            in1=HG[:, m, c0:c0 + n], op0=ALU.mult, op1=ALU.add),
                            [res(f"PG{m}_{ti}"), rHG(m, ti), res("carry")], [rS[m]])
                rms_stats(lambda c: S[c][:, :n], 4, n, lambda c: rS[c], 512 * EPS)
                for m in range(4):
                    P.op("vector", I("scalar_tensor_tensor",
                        out=YN(m, n), in0=S[m][:, :n], scalar=dcol(d0 + 20 + m), in1=rstd[:, :n],
                        op0=ALU.mult, op1=ALU.mult), [rS[m], res("rstd"), rprd], [rYN(m)])
                for mo in range(8):
                    pt, pr = bank()
                    fns = []
                    for kc in range(8):
                        rhs = YPn[:, kc, c0:c0 + n] if kc < 4 else YN(kc - 4, n)
                        fns.append(I("matmul",
                            out=pt[:, :n], lhsT=W1[:, kc, mo * 128:(mo + 1) * 128], rhs=rhs,
                            start=(kc == 0), stop=(kc == 7)))
                    P.group("tensor", fns, [res("W1")] + [res(f"YPn{m}_{ti}") for m in range(4)] +
                            [rS[4], rS[5]], [pr])
                    P.op("vector", I("tensor_tensor",
                        out=H[:, mo, c0:c0 + n], in0=pt[:, :n], in1=H[:, mo, c0:c0 + n], op=ALU.add),
                        [pr, rH(mo, ti)], [rH(mo, ti)])
            if l + 1 < nlayers:
                load_w1(w_in_d[l + 1].rearrange("(k p) n -> p k n", p=128), 1536)

            transfer(allXN, allHG)
            for ti, (c0, n) in enumerate(TILES):
                rms_stats(lambda c: H[:, c, c0:c0 + n], 8, n, lambda c: rH(c, ti), D * EPS)
                for c in range(8):
                    P.op("vector", I("scalar_tensor_tensor",
                        out=XN(c, c0, n), in0=H[:, c, c0:c0 + n], scalar=dcol(d0 + 8 + c), in1=rstd[:, :n],
                        op0=ALU.mult, op1=ALU.mult), [rH(c, ti), res("rstd"), rprd], [rXN(c, ti)])

            transfer([rhid[0]], rxb[0:4])
            transfer([rhid[1]], rxb[4:8])
            for i in range(2):
                transfer([rring[i]], rS[7 * i:7 * i + 7])
            for g in range(8):
                rg = RING[g % 2]
                rr = rring[g % 2]
                dn = f"ring{g % 2}"
                P.dma("gpsimd", [I("dma_start",
                    out=rg[:, 0:4096].rearrange("p (k f) -> p k f", k=8),
                    in_=w_up_d[l].rearrange("(k p) f -> p k f", p=128)[:, :, g * 512:(g + 1) * 512]),
                    I("dma_start",
                    out=rg[:, 4096:8192].rearrange("p (k n) -> p k n", k=4),
                    in_=w_down_d[l][g * 512:(g + 1) * 512, :].rearrange("(k p) n -> p k n", p=128))],
                    [], [rr], dn)
                for ti, (c0, n) in enumerate(TILES):
                    hj = (g * 5 + ti) % 2
                    hb_ = hid[hj]
                    for fc in range(4):
                        pt, pr = bank()
                        P.group("tensor", [
                            (I("matmul",
                                out=pt[:, :n], lhsT=rg[:, kc * 512 + fc * 128:kc * 512 + (fc + 1) * 128],
                                rhs=XN(kc, c0, n), start=(kc == 0), stop=(kc == 7)))
                            for kc in range(8)], [rr] + [rXN(c, ti) for c in range(8)], [pr])
                        P.op("scalar", I("activation", out=hb_[:, fc, :n], in_=pt[:, :n], func=AF.Relu),
                             [pr], [rhid[hj]])
                        P.op("vector", I("tensor_tensor", out=hb_[:, fc, :n], in0=hb_[:, fc, :n],
                                         in1=hb_[:, fc, :n], op=ALU.mult), [rhid[hj]], [rhid[hj]])
                    for mo in range(8):
                        pt, pr = bank()
                        P.group("tensor", [
                            (I("matmul",
                                out=pt[:, :n], lhsT=rg[:, 4096 + fc * 1024 + mo * 128:4096 + fc * 1024 + (mo + 1) * 128],
                                rhs=hb_[:, fc, :n], start=(fc == 0), stop=(fc == 3)))
                            for fc in range(4)], [rr, rhid[hj]], [pr])
                        P.op("vector", I("tensor_tensor",
                            out=H[:, mo, c0:c0 + n], in0=pt[:, :n], in1=H[:, mo, c0:c0 + n], op=ALU.add),
                            [pr, rH(mo, ti)], [rH(mo, ti)])

        for i in range(2):
            transfer(rS[7 * i:7 * i + 7], [rring[i]])
        outv = outT.rearrange("(c p) t -> p c t", p=128)
        for ti, (c0, n) in enumerate(TILES):
            if ti == 0:
                continue
            rms_stats(lambda c: H[:, c, c0:c0 + n], 8, n, lambda c: rH(c, ti), D * EPS)
            for c in range(8):
                P.op("vector", I("scalar_tensor_tensor",
                    out=S[c][:, :n], in0=H[:, c, c0:c0 + n], scalar=dcol(DFIN + c), in1=rstd[:, :n],
                    op0=ALU.mult, op1=ALU.mult), [rH(c, ti), res("rstd"), rprd], [rS[c]])
                P.dma("sync", I("dma_start",
                    out=outv[:, c, c0 - 16:c0 - 16 + n], in_=S[c][:, :n]),
                    [rS[c]], [res(f"outT{c}")], f"out{c}")
        P.wait_all("sync", [res(f"outT{c}") for c in range(8)])

        with nc.Block() as block:
            P.emit(block)
    return nc


def _prep_inputs(inputs):
    f = lambda k: np.ascontiguousarray(np.asarray(inputs[k], dtype=np.float32))
    x = f("x")
    meta = f("meta_tokens")
    pcols = np.zeros((128, NP), np.float32)
    chunks = lambda v: v.reshape(-1, 128).T
    for l in range(NLAYERS):
        b0 = l * PL
        pcols[:, b0 + 0:b0 + 8] = chunks(f("mix_norm_g")[l])
        pcols[:, b0 + 8:b0 + 16] = chunks(f("mlp_norm_g")[l])
        pcols[:, b0 + 16:b0 + 24] = chunks(f("group_norm_g")[l])
        pcols[:, b0 + 24:b0 + 28] = chunks(f("pool_b")[l])
        pcols[:, b0 + 28:b0 + 32] = chunks(f("pool_scale")[l])
        for k in range(4):
            pcols[:, b0 + 32 + 4 * k:b0 + 36 + 4 * k] = chunks(f("conv_w")[l, k])
        pcols[:, b0 + 48:b0 + 52] = chunks(f("conv_b")[l])
        pcols[:, b0 + 52:b0 + 56] = chunks(f("gate_r_b")[l])
        pcols[:, b0 + 56:b0 + 60] = chunks(f("gate_i_b")[l])
        pcols[:, b0 + 60:b0 + 64] = chunks(f("lru_lambda")[l])
    pcols[:, P_FIN:P_FIN + 8] = chunks(f("final_norm_g"))
    in_maps = []
    shared = {k: f(k) for k in ("w_in", "pool_w", "gate_r_w", "gate_i_w", "w_out", "w_up", "w_down")}
    for core in range(8):
        b, c = core // 4, core % 4
        xT = np.zeros((D, NT), np.float32)
        if c == 0:
            xT[:, :NPRE] = meta.T
        xT[:, NPRE:] = x[b, c * NMAIN:(c + 1) * NMAIN, :].T
        pc = pcols.copy()
        if c > 0:
            pc[:, P_SEL + c - 1] = 1.0
        for r in range(4):
            pc[:, P_MPREV + r] = 1.0 if r < c else 0.0
        pc[:, P_M0] = 1.0 if c == 0 else 0.0
        for g in range(4):
            k = 2 << g
            for t in range(16):
                pc[:, P_INVC + 16 * g + t] = 1.0 / (min(t + 1, k) if c == 0 else k)
        m = {"xT": xT, "prm": pc}
        m.update(shared)
        in_maps.append(m)
    return in_maps


_NC_CACHE = {}


def kernel(**inputs):
    nl = NLAYERS
    if nl not in _NC_CACHE:
        _NC_CACHE[nl] = build_nc(nl)
    nc = _NC_CACHE[nl]
    in_maps = _prep_inputs(inputs)
    res = run_bass_kernel_spmd(nc, in_maps, core_ids=list(range(8)))
    out = np.empty((2, 4 * NMAIN, D), np.float32)
    for core in range(8):
        b, c = core // 4, core % 4
        out[b, c * NMAIN:(c + 1) * NMAIN, :] = res.results[core]["outT"].T
    return out
```

```python
import numpy as np
from contextlib import ExitStack
import concourse.bass as bass
import concourse.mybir as mybir
from concourse.bass_utils import run_bass_kernel_spmd

F32 = mybir.dt.float32
BF16 = mybir.dt.bfloat16
ALU = mybir.AluOpType
AF = mybir.ActivationFunctionType

D = 1024
DFF = 4096
NPRE = 16
NMAIN = 2048
NT = NPRE + NMAIN
NLAYERS = 4
EPS = 1e-6
PL = 64
P_FIN = NLAYERS * PL
P_SEL = P_FIN + 8
P_MPREV = P_SEL + 4
P_M0 = P_MPREV + 4
P_INVC = P_M0 + 4
NP = P_INVC + 64
TILES = [(0, NPRE)] + [(NPRE + 512 * i, 512) for i in range(4)]
GELU_C1 = 0.7978845608028654
GELU_C2 = 0.044715


def I(name, *a, **kw):
    return lambda e: getattr(e, name)(*a, **kw)


class Res:
    __slots__ = ("name", "w", "rs")

    def __init__(self, name):
        self.name = name
        self.w = None
        self.rs = []


class Prog:
    def __init__(self, nc, stack):
        self.nc = nc
        self.stack = stack
        self.eng = {}
        for name in ("vector", "scalar", "gpsimd", "tensor", "sync"):
            sem = stack.enter_context(nc.semaphore("e_" + name))
            self.eng[name] = dict(sem=sem, cnt=0, seen={}, ops=[])
        self.dsem = {}

    def new_dsem(self, name):
        sem = self.stack.enter_context(self.nc.semaphore("d_" + name))
        self.dsem[name] = dict(sem=sem, cnt=0)
        return name

    def _waits(self, ename, reads, writes):
        e = self.eng[ename]
        deps = []
        for r in reads:
            if r.w is not None:
                deps.append(r.w + ("RAW",))
        for w in writes:
            if w.w is not None:
                deps.append(w.w + ("WAW",))
            for s in w.rs:
                deps.append(s + ("WAR",))
        need = {}
        for (skey, sem, val, src, kind) in deps:
            if src == ename and ename == "tensor":
                continue
            if e["seen"].get(skey, 0) >= val:
                continue
            if skey not in need or need[skey][1] < val:
                need[skey] = (sem, val)
        waits = []
        for skey, (sem, val) in need.items():
            e["seen"][skey] = val
            waits.append((sem, val))
        return waits

    def _stamp(self, stamp, reads, writes):
        for w in writes:
            w.w = stamp
            w.rs = []
        for r in reads:
            r.rs.append(stamp)

    def op(self, ename, fn, reads=(), writes=()):
        self.group(ename, [fn], reads, writes)

    def group(self, ename, fns, reads=(), writes=()):
        e = self.eng[ename]
        waits = self._waits(ename, reads, writes)
        e["cnt"] += 1
        for i, fn in enumerate(fns):
            e["ops"].append((waits if i == 0 else [], fn, (e["sem"], 1) if i == len(fns) - 1 else None))
        self._stamp(("e_" + ename, e["sem"], e["cnt"], ename), reads, writes)

    def dma(self, qname, fn, reads, writes, dname, amt=16):
        e = self.eng[qname]
        d = self.dsem[dname]
        fns = fn if isinstance(fn, list) else [fn]
        waits = self._waits(qname, reads, writes)
        for i, f in enumerate(fns):
            d["cnt"] += amt
            e["ops"].append((waits if i == 0 else [], f, (d["sem"], amt)))
        self._stamp(("d_" + dname, d["sem"], d["cnt"], None), reads, writes)

    def wait_all(self, qname, resources):
        e = self.eng[qname]
        waits = self._waits(qname, resources, ())
        if waits:
            e["ops"].append((waits, None, None))

    def emit(self, block):
        def run(ename):
            def body(eng):
                for (waits, fn, inc) in self.eng[ename]["ops"]:
                    for (sem, val) in waits:
                        eng.wait_ge(sem, val)
                    if fn is None:
                        continue
                    ins = fn(eng)
                    if inc is not None:
                        ins.then_inc(inc[0], inc[1])
            return body
        block.sync(run("sync"))
        block.gpsimd(run("gpsimd"))
        block.tensor(run("tensor"))
        block.scalar(run("scalar"))
        block.vector(run("vector"))


def build_nc(nlayers=NLAYERS):
    nc = bass.Bass("TRN2", target_bir_lowering=False)
    dt_in = lambda name, shape: nc.dram_tensor(name, shape, F32, kind="ExternalInput").ap()
    xT = dt_in("xT", [D, NT])
    prm_d = dt_in("prm", [128, NP])
    w_in_d = dt_in("w_in", [NLAYERS, D, 1536])
    pool_w_d = dt_in("pool_w", [NLAYERS, 4, 128, 128])
    gr_d = dt_in("gate_r_w", [NLAYERS, 8, 64, 64])
    gi_d = dt_in("gate_i_w", [NLAYERS, 8, 64, 64])
    w_out_d = dt_in("w_out", [NLAYERS, D, D])
    w_up_d = dt_in("w_up", [NLAYERS, D, DFF])
    w_down_d = dt_in("w_down", [NLAYERS, DFF, D])
    outT = nc.dram_tensor("outT", [D, NMAIN], F32, kind="ExternalOutput").ap()
    halo_src = [nc.dram_tensor(f"halo_src{l}", [128, 128], F32) for l in range(nlayers)]
    halo_dst = [nc.dram_tensor(f"halo_dst{l}", [512, 128], F32) for l in range(nlayers)]
    sum_src = [nc.dram_tensor(f"sum_src{l}", [128, 8], F32) for l in range(nlayers)]
    sum_dst = [nc.dram_tensor(f"sum_dst{l}", [512, 8], F32) for l in range(nlayers)]
    groups = [[0, 1, 2, 3], [4, 5, 6, 7]]

    with ExitStack() as st:
        def sb(name, shape, dt):
            return st.enter_context(nc.sbuf_tensor(name, shape, dt))

        P = Prog(nc, st)
        H = sb("H", [128, 8, NT], F32)
        HG = sb("HG", [128, 4, NT], F32)
        XNv = HG[:].bitcast(BF16)
        PG = sb("PG", [128, 4, NMAIN], BF16)
        YPn = sb("YPn", [128, 4, NT], BF16)
        W1 = sb("W1", [128, 8, 1536], BF16)
        RING = [sb(f"ring{i}", [128, 8192], BF16) for i in range(2)]
        RINGF = [r[:].bitcast(F32) for r in RING]
        S = [RINGF[j // 7][:, (j % 7) * 528:(j % 7 + 1) * 528] for j in range(14)]
        prm = sb("prm_sb", [128, NP], F32)
        prd = sb("prd", [128, NLAYERS * 48 + 8], F32)
        ones = sb("ones", [128, 128], BF16)
        pw = sb("pw", [128, 4, 128], BF16)
        grw = sb("grw", [128, 4, 128], BF16)
        giw = sb("giw", [128, 4, 128], BF16)
        sq = sb("sq", [128, 2, 512], BF16)
        xb = sb("xb", [128, 8, 512], BF16)
        hid = [xb[:, 0:4, :], xb[:, 4:8, :]]
        rstd = sb("rstd", [128, 512], F32)
        pooled = sb("pooled", [128, 2, 512], BF16)
        xcb = sb("xcb", [128, 512], BF16)
        UPh = sb("UPh", [128, 4, 16], F32)
        URh = sb("URh", [128, 4, 16], F32)
        state = sb("state", [128, 8], F32)
        hsrc = sb("hsrc", [128, 8, 16], F32)
        hb = sb("hb", [128, 8, 16], F32)
        srecv = sb("srecv", [128, 4, 8], F32)
        carry = sb("carry", [128, 4], F32)
        ctmp = sb("ctmp", [128, 4], F32)
        ps = [st.enter_context(nc.psum_tensor(f"ps{i}", [128, 512], F32)) for i in range(8)]

        R = {}

        def res(name):
            if name not in R:
                R[name] = Res(name)
            return R[name]

        def transfer(dsts, srcs):
            for d in dsts:
                for s in srcs:
                    if s.w is not None:
                        d.rs.append(s.w)
                    d.rs.extend(s.rs)

        rH = lambda c, t: res(f"H{c}_{t}")
        rS = [res(f"S{j}") for j in range(14)]
        rring = [res("ring0"), res("ring1")]
        rxb = [res(f"xb{c}") for c in range(8)]
        rhid = [res("hid0"), res("hid1")]
        rps = [res(f"ps{i}") for i in range(8)]
        bank_i = [0]

        def bank():
            i = bank_i[0] % 8
            bank_i[0] += 1
            return ps[i], rps[i]

        for n_ in (["ld_prm", "w1", "pw", "grw", "giw", "ring0", "ring1", "halo_o", "halo_cc", "halo_i",
                    "sum_o", "sum_cc", "sum_i"] + [f"ld_x{i}" for i in range(5)] + [f"out{i}" for i in range(8)]):
            P.new_dsem(n_)

        P.dma("sync", I("dma_start", out=prm[:], in_=prm_d), [], [res("prm")], "ld_prm")
        for ti, (c0, n) in enumerate(TILES):
            P.dma("sync", I("dma_start",
                out=H[:, :, c0:c0 + n], in_=xT.rearrange("(c p) t -> p c t", p=128)[:, :, c0:c0 + n]),
                [], [rH(c, ti) for c in range(8)], f"ld_x{ti}")
        P.op("vector", I("memset", ones[:], 1.0), [], [res("ones")])
        P.op("vector", I("memset", grw[:], 0.0), [], [res("grw")])
        P.op("vector", I("memset", giw[:], 0.0), [], [res("giw")])

        rprd = res("prd")
        rprm = res("prm")
        DB = lambda l: l * 48
        for l in range(NLAYERS):
            b0, d0 = l * PL, DB(l)
            P.op("vector", I("tensor_scalar",
                out=prd[:, d0:d0 + 16], in0=prm[:, b0:b0 + 16], scalar1=32.0, scalar2=None, op0=ALU.mult),
                [rprm], [rprd])
            P.op("vector", I("tensor_scalar",
                out=prd[:, d0 + 16:d0 + 24], in0=prm[:, b0 + 16:b0 + 24], scalar1=float(np.sqrt(512.0)),
                scalar2=None, op0=ALU.mult), [rprm], [rprd])
            P.op("vector", I("tensor_scalar",
                out=prd[:, d0 + 24:d0 + 32], in0=prm[:, b0 + 52:b0 + 60], scalar1=-1.0, scalar2=None,
                op0=ALU.mult), [rprm], [rprd])
            P.op("scalar", I("activation",
                out=prd[:, d0 + 40:d0 + 44], in_=prm[:, b0 + 60:b0 + 64], func=AF.Exp, scale=-1.0),
                [rprm], [rprd])
        for l in range(NLAYERS):
            d0 = DB(l)
            P.op("scalar", I("activation",
                out=prd[:, d0 + 44:d0 + 48], in_=prd[:, d0 + 40:d0 + 44], func=AF.Ln, bias=1.0, scale=1.0),
                [rprd], [rprd])
            P.op("vector", I("tensor_scalar",
                out=prd[:, d0 + 32:d0 + 36], in0=prd[:, d0 + 44:d0 + 48], scalar1=-8.0, scalar2=None,
                op0=ALU.mult), [rprd], [rprd])
            P.op("vector", I("tensor_scalar",
                out=prd[:, d0 + 36:d0 + 40], in0=prd[:, d0 + 44:d0 + 48], scalar1=-16.0, scalar2=None,
                op0=ALU.mult), [rprd], [rprd])
        DFIN = NLAYERS * 48
        P.op("vector", I("tensor_scalar",
            out=prd[:, DFIN:DFIN + 8], in0=prm[:, P_FIN:P_FIN + 8], scalar1=32.0, scalar2=None, op0=ALU.mult),
            [rprm], [rprd])

        pcol = lambda j: prm[:, j:j + 1]
        dcol = lambda j: prd[:, j:j + 1]
        sq_i = [0]

        def rms_stats(src_fn, nchunks, n, src_res, add_const):
            pt, pr = bank()
            for c in range(nchunks):
                j = sq_i[0] % 2
                sq_i[0] += 1
                P.op("scalar", I("activation", out=sq[:, j, :n], in_=src_fn(c), func=AF.Square),
                     [src_res(c)], [res(f"sq{j}")])
                P.op("tensor", I("matmul", out=pt[:, :n], lhsT=ones[:], rhs=sq[:, j, :n],
                                                           start=(c == 0), stop=(c == nchunks - 1)),
                     [res("ones"), res(f"sq{j}")], [pr])
            P.op("vector", I("tensor_scalar", out=rstd[:, :n], in0=pt[:, :n], scalar1=float(add_const),
                             scalar2=None, op0=ALU.add), [pr], [res("rstd")])
            P.op("scalar", I("activation", out=rstd[:, :n], in_=rstd[:, :n], func=AF.Ln), [res("rstd")], [res("rstd")])
            P.op("scalar", I("activation", out=rstd[:, :n], in_=rstd[:, :n], func=AF.Exp, scale=-0.5),
                 [res("rstd")], [res("rstd")])

        def load_w1(src3, ncols):
            P.dma("gpsimd", [I("dma_start", out=W1[:, k:k + 2, :ncols], in_=src3[:, k:k + 2, :])
                             for k in range(0, 8, 2)], [], [res("W1")], "w1")

        def load_small(l):
            P.dma("gpsimd", I("dma_start", out=pw[:], in_=pool_w_d[l].rearrange("g i j -> i g j")),
                  [], [res("pw")], "pw")
            for (dst, srcd, rn) in ((grw, gr_d, "grw"), (giw, gi_d, "giw")):
                s4 = srcd[l].rearrange("(q two) i j -> two i q j", two=2)
                P.dma("gpsimd", [I("dma_start", out=dst[64 * b:64 * b + 64, :, 64 * b:64 * b + 64], in_=s4[b])
                                 for b in range(2)], [], [res(rn)], rn)

        def inproj_norm(c0, n, ti, gbase):
            rms_stats(lambda c: H[:, c, c0:c0 + n], 8, n, lambda c: rH(c, ti), D * EPS)
            for c in range(8):
                P.op("vector", I("scalar_tensor_tensor",
                    out=xb[:, c, :n], in0=H[:, c, c0:c0 + n], scalar=dcol(gbase + c), in1=rstd[:, :n],
                    op0=ALU.mult, op1=ALU.mult), [rH(c, ti), res("rstd"), rprd], [rxb[c]])

        def inproj_mm(m, n):
            pt, pr = bank()
            P.group("tensor", [
                (I("matmul", out=pt[:, :n], lhsT=W1[:, kc, m * 128:(m + 1) * 128], rhs=xb[:, kc, :n],
                                           start=(kc == 0), stop=(kc == 7)))
                for kc in range(8)], [res("W1")] + rxb, [pr])
            return pt, pr

        rHG = lambda m, ti: res(f"HG{m}_{ti}")
        rXN = lambda c, ti: res(f"XN{c}_{ti}")
        allHG = [rHG(m, ti) for m in range(4) for ti in range(5)]
        allXN = [rXN(c, ti) for c in range(8) for ti in range(5)]
        XN = lambda c, c0, n: XNv[:, c // 2, (c % 2) * NT + c0:(c % 2) * NT + c0 + n]

        load_w1(w_in_d[0].rearrange("(k p) n -> p k n", p=128), 1536)
        for l in range(nlayers):
            b0, d0 = l * PL, DB(l)
            load_small(l)
            if l > 0:
                for i in range(2):
                    transfer(rS[7 * i:7 * i + 7], [rring[i]])
                transfer(rxb[0:4], [rhid[0]])
                transfer(rxb[4:8], [rhid[1]])
                transfer(allHG, allXN)
            P.op("vector", I("memset", state[:, 0:4], 0.0), [], [res("state")])
            P.op("vector", I("memset", state[:, 4:8], 1.0), [], [res("state")])
            P.op("vector", I("memset", UPh[:], 0.0), [], [res("UPh")])
            P.op("vector", I("memset", URh[:], 0.0), [], [res("URh")])

            c0t = NT - 16
            inproj_norm(c0t, 16, 4, d0 + 0)
            for m in range(8):
                pt, pr = inproj_mm(m, 16)
                P.op("scalar", I("copy", out=hsrc[:, m, :], in_=pt[:, :16]), [pr], [res("hsrc")])
            P.dma("sync", I("dma_start", out=halo_src[l].ap(), in_=hsrc[:].rearrange("p m j -> p (m j)")),
                  [res("hsrc")], [res(f"halo_src{l}")], "halo_o")
            P.dma("gpsimd", I("collective_compute",
                "AllGather", ALU.bypass, replica_groups=groups, ins=[halo_src[l].ap()], outs=[halo_dst[l].ap()]),
                [res(f"halo_src{l}")], [res(f"halo_dst{l}")], "halo_cc", amt=1)
            hrecv = S[6][:, 0:512].rearrange("p (r f) -> p r f", r=4)
            P.dma("sync", I("dma_start",
                out=hrecv, in_=halo_dst[l].ap().rearrange("(r p) f -> p r f", p=128)),
                [res(f"halo_dst{l}")], [rS[6]], "halo_i")
            hbf = hb[:].rearrange("p m j -> p (m j)")
            P.op("vector", I("tensor_scalar", out=hbf, in0=hrecv[:, 0, :], scalar1=pcol(P_SEL), scalar2=None,
                                                     op0=ALU.mult), [rS[6], rprm], [res("hb")])
            for r in range(1, 4):
                P.op("vector", I("scalar_tensor_tensor",
                    out=hbf, in0=hrecv[:, r, :], scalar=pcol(P_SEL + r), in1=hbf, op0=ALU.mult, op1=ALU.add),
                    [rS[6], res("hb"), rprm], [res("hb")])

            for ti, (c0, n) in enumerate(TILES):
                is_pre = (ti == 0)
                W = 16 + n
                inproj_norm(c0, n, ti, d0 + 0)
                for m in range(4):
                    k = 2 << m
                    pt, pr = inproj_mm(m, n)
                    P.op("vector", I("tensor_copy", out=S[0][:, 0:16], in_=UPh[:, m, :]),
                         [res("UPh")], [rS[0]])
                    P.op("scalar", I("copy", out=S[0][:, 16:W], in_=pt[:, :n]), [pr], [rS[0]])
                    if is_pre:
                        P.op("vector", I("scalar_tensor_tensor",
                            out=S[0][:, 16:32], in0=S[0][:, 16:32], scalar=pcol(P_M0), in1=hb[:, m, :],
                            op0=ALU.mult, op1=ALU.add), [rS[0], res("hb"), rprm], [rS[0]])
                    P.op("vector", I("tensor_copy", out=UPh[:, m, :], in_=S[0][:, n:W]),
                         [rS[0]], [res("UPh")])
                    cur, curres = 0, rS[0]
                    step = 1
                    bi = 1
                    while step < k:
                        lo = 2 * step - 1
                        P.op("vector", I("tensor_tensor",
                            out=S[bi][:, lo:W], in0=S[cur][:, lo:W], in1=S[cur][:, lo - step:W - step], op=ALU.add),
                            [curres], [rS[bi]])
                        cur, curres = bi, rS[bi]
                        step *= 2
                        bi = 3 - bi
                    pj = m % 2
                    if is_pre:
                        o = 3 - cur
                        P.op("vector", I("tensor_tensor",
                            out=S[o][:, 0:16], in0=S[cur][:, 16:32],
                            in1=prm[:, P_INVC + 16 * m:P_INVC + 16 * m + 16], op=ALU.mult),
                            [curres, rprm], [rS[o]])
                        P.op("vector", I("tensor_tensor",
                            out=pooled[:, pj, :16], in0=S[o][:, 0:16], in1=S[0][:, 16:32], op=ALU.subtract),
                            [rS[o], rS[0]], [res(f"pooled{pj}")])
                    else:
                        P.op("vector", I("scalar_tensor_tensor",
                            out=pooled[:, pj, :n], in0=S[cur][:, 16:W], scalar=1.0 / k, in1=S[0][:, 16:W],
                            op0=ALU.mult, op1=ALU.subtract), [curres, rS[0]], [res(f"pooled{pj}")])
                    pt2, pr2 = bank()
                    P.op("tensor", I("matmul",
                        out=pt2[:, :n], lhsT=pw[:, m, :], rhs=pooled[:, pj, :n], start=True, stop=True),
                        [res("pw"), res(f"pooled{pj}")], [pr2])
                    P.op("vector", I("tensor_scalar",
                        out=S[3 + m][:, :n], in0=pt2[:, :n], scalar1=pcol(b0 + 24 + m), scalar2=pcol(b0 + 28 + m),
                        op0=ALU.add, op1=ALU.mult), [pr2, rprm], [rS[3 + m]])
                rms_stats(lambda c: S[3 + c][:, :n], 4, n, lambda c: rS[3 + c], 512 * EPS)
                for m in range(4):
                    P.op("vector", I("scalar_tensor_tensor",
                        out=YPn[:, m, c0:c0 + n], in0=S[3 + m][:, :n], scalar=dcol(d0 + 16 + m), in1=rstd[:, :n],
                        op0=ALU.mult, op1=ALU.mult), [rS[3 + m], res("rstd"), rprd], [res(f"YPn{m}_{ti}")])

                for m in range(4):
                    pt, pr = inproj_mm(4 + m, n)
                    P.op("vector", I("tensor_copy", out=S[7][:, 0:16], in_=URh[:, m, :]),
                         [res("URh")], [rS[7]])
                    P.op("scalar", I("copy", out=S[7][:, 16:W], in_=pt[:, :n]), [pr], [rS[7]])
                    if is_pre:
                        P.op("vector", I("scalar_tensor_tensor",
                            out=S[7][:, 16:32], in0=S[7][:, 16:32], scalar=pcol(P_M0), in1=hb[:, 4 + m, :],
                            op0=ALU.mult, op1=ALU.add), [rS[7], res("hb"), rprm], [rS[7]])
                    P.op("vector", I("tensor_copy", out=URh[:, m, :], in_=S[7][:, n:W]),
                         [rS[7]], [res("URh")])
                    ptg, prg = inproj_mm(8 + m, n)
                    P.op("vector", I("tensor_copy", out=S[9][:, :n], in_=ptg[:, :n]), [prg], [rS[9]])
                    P.op("scalar", I("activation", out=S[10][:, :n], in_=ptg[:, :n], func=AF.Square),
                         [prg], [rS[10]])
                    P.op("vector", I("tensor_scalar",
                        out=S[8][:, :n], in0=S[7][:, 13:13 + n], scalar1=pcol(b0 + 32 + m), scalar2=pcol(b0 + 48 + m),
                        op0=ALU.mult, op1=ALU.add), [rS[7], rprm], [rS[8]])
                    for k in range(1, 4):
                        P.op("vector", I("scalar_tensor_tensor",
                            out=S[8][:, :n], in0=S[7][:, 13 + k:13 + k + n], scalar=pcol(b0 + 32 + 4 * k + m),
                            in1=S[8][:, :n], op0=ALU.mult, op1=ALU.add), [rS[7], rS[8], rprm], [rS[8]])
                    P.op("scalar", I("copy", out=xcb[:, :n], in_=S[8][:, :n]), [rS[8]], [res("xcb")])
                    ptr, prr = bank()
                    P.op("tensor", I("matmul", out=ptr[:, :n], lhsT=grw[:, m, :],
                                                                   rhs=xcb[:, :n], start=True, stop=True),
                         [res("grw"), res("xcb")], [prr])
                    pti, pri = bank()
                    P.op("tensor", I("matmul", out=pti[:, :n], lhsT=giw[:, m, :],
                                                                   rhs=xcb[:, :n], start=True, stop=True),
                         [res("giw"), res("xcb")], [pri])
                    def sigmoid_chain(dst, dres, src_ap, sres, scale, bias):
                        P.op("scalar", I("activation", out=dst, in_=src_ap, func=AF.Exp, bias=bias, scale=scale),
                             sres + [rprd], [dres])
                        P.op("scalar", I("activation", out=dst, in_=dst, func=AF.Ln, bias=1.0, scale=1.0),
                             [dres], [dres])
                        P.op("scalar", I("activation", out=dst, in_=dst, func=AF.Exp, scale=-1.0), [dres], [dres])
                    sigmoid_chain(S[11][:, :n], rS[11], ptr[:, :n], [prr], -1.0, dcol(d0 + 24 + m))
                    sigmoid_chain(S[12][:, :n], rS[12], pti[:, :n], [pri], -1.0, dcol(d0 + 28 + m))
                    P.op("scalar", I("activation", out=S[1][:, :n], in_=S[11][:, :n], func=AF.Exp,
                                     scale=dcol(d0 + 32 + m)), [rS[11], rprd], [rS[1]])
                    P.op("scalar", I("activation", out=S[13][:, :n], in_=S[11][:, :n], func=AF.Exp,
                                     scale=dcol(d0 + 36 + m)), [rS[11], rprd], [rS[13]])
                    P.op("vector", I("tensor_scalar", out=S[13][:, :n], in0=S[13][:, :n], scalar1=1.0, scalar2=-1.0,
                                     op0=ALU.min, op1=ALU.mult), [rS[13]], [rS[13]])
                    P.op("scalar", I("activation", out=S[13][:, :n], in_=S[13][:, :n], func=AF.Ln, bias=1.0,
                                     scale=1.0), [rS[13]], [rS[13]])
                    P.op("scalar", I("activation", out=S[13][:, :n], in_=S[13][:, :n], func=AF.Exp, scale=0.5),
                         [rS[13]], [rS[13]])
                    P.op("vector", I("tensor_tensor", out=S[12][:, :n], in0=S[12][:, :n], in1=S[8][:, :n],
                                     op=ALU.mult), [rS[12], rS[8]], [rS[12]])
                    P.op("vector", I("tensor_tensor", out=S[12][:, :n], in0=S[13][:, :n], in1=S[12][:, :n],
                                     op=ALU.mult), [rS[13], rS[12]], [rS[12]])
                    P.op("vector", I("tensor_tensor_scan",
                        out=S[2][:, :n], data0=S[1][:, :n], data1=S[12][:, :n], initial=state[:, m:m + 1],
                        op0=ALU.mult, op1=ALU.add), [rS[1], rS[12], res("state")], [rS[2]])
                    if is_pre:
                        P.op("vector", I("tensor_scalar",
                            out=state[:, m:m + 1], in0=S[2][:, n - 1:n], scalar1=pcol(P_M0), scalar2=None, op0=ALU.mult),
                            [rS[2], rprm], [res("state")])
                    else:
                        P.op("vector", I("tensor_copy", out=state[:, m:m + 1], in_=S[2][:, n - 1:n]),
                             [rS[2]], [res("state")])
                        P.op("vector", I("tensor_tensor_scan",
                            out=S[0][:, :n], data0=S[1][:, :n], data1=S[1][:, :n], initial=state[:, 4 + m:5 + m],
                            op0=ALU.mult, op1=ALU.min), [rS[1], res("state")], [rS[0]])
                        P.op("vector", I("tensor_copy", out=state[:, 4 + m:5 + m], in_=S[0][:, n - 1:n]),
                             [rS[0]], [res("state")])
                    P.op("vector", I("tensor_scalar",
                        out=S[10][:, :n], in0=S[10][:, :n], scalar1=GELU_C2, scalar2=1.0, op0=ALU.mult, op1=ALU.add),
                        [rS[10]], [rS[10]])
                    P.op("vector", I("tensor_tensor",
                        out=S[10][:, :n], in0=S[10][:, :n], in1=S[9][:, :n], op=ALU.mult),
                        [rS[10], rS[9]], [rS[10]])
                    sigmoid_chain(S[10][:, :n], rS[10], S[10][:, :n], [rS[10]], -2.0 * GELU_C1, 0.0)
                    P.op("vector", I("tensor_tensor", out=S[10][:, :n], in0=S[10][:, :n], in1=S[9][:, :n],
                                     op=ALU.mult), [rS[10], rS[9]], [rS[10]])
                    P.op("vector", I("tensor_tensor", out=HG[:, m, c0:c0 + n], in0=S[2][:, :n], in1=S[10][:, :n],
                                     op=ALU.mult), [rS[2], rS[10]], [rHG(m, ti)])
                    if not is_pre:
                        P.op("vector", I("tensor_tensor", out=PG[:, m, c0 - 16:c0 - 16 + n], in0=S[0][:, :n],
                                         in1=S[10][:, :n], op=ALU.mult), [rS[0], rS[10]], [res(f"PG{m}_{ti}")])

            P.dma("sync", I("dma_start", out=sum_src[l].ap(), in_=state[:]),
                  [res("state")], [res(f"sum_src{l}")], "sum_o")
            P.dma("gpsimd", I("collective_compute",
                "AllGather", ALU.bypass, replica_groups=groups, ins=[sum_src[l].ap()], outs=[sum_dst[l].ap()]),
                [res(f"sum_src{l}")], [res(f"sum_dst{l}")], "sum_cc", amt=1)
            P.dma("sync", I("dma_start",
                out=srecv[:], in_=sum_dst[l].ap().rearrange("(r p) f -> p r f", p=128)),
                [res(f"sum_dst{l}")], [res("srecv")], "sum_i")
            load_w1(w_out_d[l].rearrange("(k p) n -> p k n", p=128), 1024)
            P.op("vector", I("memset", carry[:], 0.0), [], [res("carry")])
            for r in range(3):
                P.op("vector", I("tensor_tensor", out=ctmp[:], in0=srecv[:, r, 4:8], in1=carry[:],
                                                            op=ALU.mult), [res("srecv"), res("carry")], [res("ctmp")])
                P.op("vector", I("tensor_tensor", out=ctmp[:], in0=ctmp[:], in1=srecv[:, r, 0:4],
                                                            op=ALU.add), [res("srecv"), res("ctmp")], [res("ctmp")])
                P.op("vector", I("tensor_tensor", out=ctmp[:], in0=ctmp[:], in1=carry[:],
                                                            op=ALU.subtract), [res("carry"), res("ctmp")], [res("ctmp")])
                P.op("vector", I("scalar_tensor_tensor",
                    out=carry[:], in0=ctmp[:], scalar=pcol(P_MPREV + r), in1=carry[:], op0=ALU.mult, op1=ALU.add),
                    [res("ctmp"), res("carry"), rprm], [res("carry")])

            ynv = [S[4].bitcast(BF16), S[5].bitcast(BF16)]
            YN = lambda m, n: ynv[m // 2][:, (m % 2) * 512:(m % 2) * 512 + n]
            rYN = lambda m: rS[4 + m // 2]
            for ti, (c0, n) in enumerate(TILES):
                is_pre = (ti == 0)
                for m in range(4):
                    if is_pre:
                        P.op("vector", I("tensor_copy", out=S[m][:, :n], in_=HG[:, m, c0:c0 + n]),
                             [rHG(m, ti)], [rS[m]])
                    else:
                        P.op("vector", I("scalar_tensor_tensor",
                            out=S[m][:, :n], in0=PG[:, m, c0 - 16:c0 - 16 + n], scalar=carry[:, m:m + 1],
                            in1=HG[:, m, c0:c0 + n], op0=ALU.mult, op1=ALU.add),
                            [res(f"PG{m}_{ti}"), rHG(m, ti), res("carry")], [rS[m]])
                rms_stats(lambda c: S[c][:, :n], 4, n, lambda c: rS[c], 512 * EPS)
                for m in range(4):
                    P.op("vector", I("scalar_tensor_tensor",
                        out=YN(m, n), in0=S[m][:, :n], scalar=dcol(d0 + 20 + m), in1=rstd[:, :n],
                        op0=ALU.mult, op1=ALU.mult), [rS[m], res("rstd"), rprd], [rYN(m)])
                for mo in range(8):
                    pt, pr = bank()
                    fns = []
                    for kc in range(8):
                        rhs = YPn[:, kc, c0:c0 + n] if kc < 4 else YN(kc - 4, n)
                        fns.append(I("matmul",
                            out=pt[:, :n], lhsT=W1[:, kc, mo * 128:(mo + 1) * 128], rhs=rhs,
                            start=(kc == 0), stop=(kc == 7)))
                    P.group("tensor", fns, [res("W1")] + [res(f"YPn{m}_{ti}") for m in range(4)] +
                            [rS[4], rS[5]], [pr])
                    P.op("vector", I("tensor_tensor",
                        out=H[:, mo, c0:c0 + n], in0=pt[:, :n], in1=H[:, mo, c0:c0 + n], op=ALU.add),
                        [pr, rH(mo, ti)], [rH(mo, ti)])
            if l + 1 < nlayers:
                load_w1(w_in_d[l + 1].rearrange("(k p) n -> p k n", p=128), 1536)

            transfer(allXN, allHG)
            for ti, (c0, n) in enumerate(TILES):
                rms_stats(lambda c: H[:, c, c0:c0 + n], 8, n, lambda c: rH(c, ti), D * EPS)
                for c in range(8):
                    P.op("vector", I("scalar_tensor_tensor",
                        out=XN(c, c0, n), in0=H[:, c, c0:c0 + n], scalar=dcol(d0 + 8 + c), in1=rstd[:, :n],
                        op0=ALU.mult, op1=ALU.mult), [rH(c, ti), res("rstd"), rprd], [rXN(c, ti)])

            transfer([rhid[0]], rxb[0:4])
            transfer([rhid[1]], rxb[4:8])
            for i in range(2):
                transfer([rring[i]], rS[7 * i:7 * i + 7])
            for g in range(8):
                rg = RING[g % 2]
                rr = rring[g % 2]
                dn = f"ring{g % 2}"
                P.dma("gpsimd", [I("dma_start",
                    out=rg[:, 0:4096].rearrange("p (k f) -> p k f", k=8),
                    in_=w_up_d[l].rearrange("(k p) f -> p k f", p=128)[:, :, g * 512:(g + 1) * 512]),
                    I("dma_start",
                    out=rg[:, 4096:8192].rearrange("p (k n) -> p k n", k=4),
                    in_=w_down_d[l][g * 512:(g + 1) * 512, :].rearrange("(k p) n -> p k n", p=128))],
                    [], [rr], dn)
                for ti, (c0, n) in enumerate(TILES):
                    hj = (g * 5 + ti) % 2
                    hb_ = hid[hj]
                    for fc in range(4):
                        pt, pr = bank()
                        P.group("tensor", [
                            (I("matmul",
                                out=pt[:, :n], lhsT=rg[:, kc * 512 + fc * 128:kc * 512 + (fc + 1) * 128],
                                rhs=XN(kc, c0, n), start=(kc == 0), stop=(kc == 7)))
                            for kc in range(8)], [rr] + [rXN(c, ti) for c in range(8)], [pr])
                        P.op("scalar", I("activation", out=hb_[:, fc, :n], in_=pt[:, :n], func=AF.Relu),
                             [pr], [rhid[hj]])
                        P.op("vector", I("tensor_tensor", out=hb_[:, fc, :n], in0=hb_[:, fc, :n],
                                         in1=hb_[:, fc, :n], op=ALU.mult), [rhid[hj]], [rhid[hj]])
                    for mo in range(8):
                        pt, pr = bank()
                        P.group("tensor", [
                            (I("matmul",
                                out=pt[:, :n], lhsT=rg[:, 4096 + fc * 1024 + mo * 128:4096 + fc * 1024 + (mo + 1) * 128],
                                rhs=hb_[:, fc, :n], start=(fc == 0), stop=(fc == 3)))
                            for fc in range(4)], [rr, rhid[hj]], [pr])
                        P.op("vector", I("tensor_tensor",
                            out=H[:, mo, c0:c0 + n], in0=pt[:, :n], in1=H[:, mo, c0:c0 + n], op=ALU.add),
                            [pr, rH(mo, ti)], [rH(mo, ti)])

        for i in range(2):
            transfer(rS[7 * i:7 * i + 7], [rring[i]])
        outv = outT.rearrange("(c p) t -> p c t", p=128)
        for ti, (c0, n) in enumerate(TILES):
            if ti == 0:
                continue
            rms_stats(lambda c: H[:, c, c0:c0 + n], 8, n, lambda c: rH(c, ti), D * EPS)
            for c in range(8):
                P.op("vector", I("scalar_tensor_tensor",
                    out=S[c][:, :n], in0=H[:, c, c0:c0 + n], scalar=dcol(DFIN + c), in1=rstd[:, :n],
                    op0=ALU.mult, op1=ALU.mult), [rH(c, ti), res("rstd"), rprd], [rS[c]])
                P.dma("sync", I("dma_start",
                    out=outv[:, c, c0 - 16:c0 - 16 + n], in_=S[c][:, :n]),
                    [rS[c]], [res(f"outT{c}")], f"out{c}")
        P.wait_all("sync", [res(f"outT{c}") for c in range(8)])

        with nc.Block() as block:
            P.emit(block)
    return nc


def _prep_inputs(inputs):
    f = lambda k: np.ascontiguousarray(np.asarray(inputs[k], dtype=np.float32))
    x = f("x")
    meta = f("meta_tokens")
    pcols = np.zeros((128, NP), np.float32)
    chunks = lambda v: v.reshape(-1, 128).T
    for l in range(NLAYERS):
        b0 = l * PL
        pcols[:, b0 + 0:b0 + 8] = chunks(f("mix_norm_g")[l])
        pcols[:, b0 + 8:b0 + 16] = chunks(f("mlp_norm_g")[l])
        pcols[:, b0 + 16:b0 + 24] = chunks(f("group_norm_g")[l])
        pcols[:, b0 + 24:b0 + 28] = chunks(f("pool_b")[l])
        pcols[:, b0 + 28:b0 + 32] = chunks(f("pool_scale")[l])
        for k in range(4):
            pcols[:, b0 + 32 + 4 * k:b0 + 36 + 4 * k] = chunks(f("conv_w")[l, k])
        pcols[:, b0 + 48:b0 + 52] = chunks(f("conv_b")[l])
        pcols[:, b0 + 52:b0 + 56] = chunks(f("gate_r_b")[l])
        pcols[:, b0 + 56:b0 + 60] = chunks(f("gate_i_b")[l])
        pcols[:, b0 + 60:b0 + 64] = chunks(f("lru_lambda")[l])
    pcols[:, P_FIN:P_FIN + 8] = chunks(f("final_norm_g"))
    in_maps = []
    shared = {k: f(k) for k in ("w_in", "pool_w", "gate_r_w", "gate_i_w", "w_out", "w_up", "w_down")}
    for core in range(8):
        b, c = core // 4, core % 4
        xT = np.zeros((D, NT), np.float32)
        if c == 0:
            xT[:, :NPRE] = meta.T
        xT[:, NPRE:] = x[b, c * NMAIN:(c + 1) * NMAIN, :].T
        pc = pcols.copy()
        if c > 0:
            pc[:, P_SEL + c - 1] = 1.0
        for r in range(4):
            pc[:, P_MPREV + r] = 1.0 if r < c else 0.0
        pc[:, P_M0] = 1.0 if c == 0 else 0.0
        for g in range(4):
            k = 2 << g
            for t in range(16):
                pc[:, P_INVC + 16 * g + t] = 1.0 / (min(t + 1, k) if c == 0 else k)
        m = {"xT": xT, "prm": pc}
        m.update(shared)
        in_maps.append(m)
    return in_maps


_NC_CACHE = {}


def kernel(**inputs):
    nl = NLAYERS
    if nl not in _NC_CACHE:
        _NC_CACHE[nl] = build_nc(nl)
    nc = _NC_CACHE[nl]
    in_maps = _prep_inputs(inputs)
    res = run_bass_kernel_spmd(nc, in_maps, core_ids=list(range(8)))
    out = np.empty((2, 4 * NMAIN, D), np.float32)
    for core in range(8):
        b, c = core // 4, core % 4
        out[b, c * NMAIN:(c + 1) * NMAIN, :] = res.results[core]["outT"].T
    return out
```

```python
import numpy as np
from contextlib import ExitStack
import concourse.bass as bass
import concourse.mybir as mybir
from concourse.bass_utils import run_bass_kernel_spmd

F32 = mybir.dt.float32
BF16 = mybir.dt.bfloat16
ALU = mybir.AluOpType
AF = mybir.ActivationFunctionType

D = 1024
DFF = 4096
NPRE = 16
NMAIN = 2048
NT = NPRE + NMAIN
NLAYERS = 4
EPS = 1e-6
PL = 64
P_FIN = NLAYERS * PL
P_SEL = P_FIN + 8
P_MPREV = P_SEL + 4
P_M0 = P_MPREV + 4
P_INVC = P_M0 + 4
NP = P_INVC + 64
TILES = [(0, NPRE)] + [(NPRE + 512 * i, 512) for i in range(4)]
GELU_C1 = 0.7978845608028654
GELU_C2 = 0.044715


def I(name, *a, **kw):
    return lambda e: getattr(e, name)(*a, **kw)


class Res:
    __slots__ = ("name", "w", "rs")

    def __init__(self, name):
        self.name = name
        self.w = None
        self.rs = []


class Prog:
    def __init__(self, nc, stack):
        self.nc = nc
        self.stack = stack
        self.eng = {}
        for name in ("vector", "scalar", "gpsimd", "tensor", "sync"):
            sem = stack.enter_context(nc.semaphore("e_" + name))
            self.eng[name] = dict(sem=sem, cnt=0, seen={}, ops=[])
        self.dsem = {}

    def new_dsem(self, name):
        sem = self.stack.enter_context(self.nc.semaphore("d_" + name))
        self.dsem[name] = dict(sem=sem, cnt=0)
        return name

    def _waits(self, ename, reads, writes):
        e = self.eng[ename]
        deps = []
        for r in reads:
            if r.w is not None:
                deps.append(r.w + ("RAW",))
        for w in writes:
            if w.w is not None:
                deps.append(w.w + ("WAW",))
            for s in w.rs:
                deps.append(s + ("WAR",))
        need = {}
        for (skey, sem, val, src, kind) in deps:
            if src == ename and ename == "tensor":
                continue
            if e["seen"].get(skey, 0) >= val:
                continue
            if skey not in need or need[skey][1] < val:
                need[skey] = (sem, val)
        waits = []
        for skey, (sem, val) in need.items():
            e["seen"][skey] = val
            waits.append((sem, val))
        return waits

    def _stamp(self, stamp, reads, writes):
        for w in writes:
            w.w = stamp
            w.rs = []
        for r in reads:
            r.rs.append(stamp)

    def op(self, ename, fn, reads=(), writes=()):
        self.group(ename, [fn], reads, writes)

    def group(self, ename, fns, reads=(), writes=()):
        e = self.eng[ename]
        waits = self._waits(ename, reads, writes)
        e["cnt"] += 1
        for i, fn in enumerate(fns):
            e["ops"].append((waits if i == 0 else [], fn, (e["sem"], 1) if i == len(fns) - 1 else None))
        self._stamp(("e_" + ename, e["sem"], e["cnt"], ename), reads, writes)

    def dma(self, qname, fn, reads, writes, dname, amt=16):
        e = self.eng[qname]
        d = self.dsem[dname]
        fns = fn if isinstance(fn, list) else [fn]
        waits = self._waits(qname, reads, writes)
        for i, f in enumerate(fns):
            d["cnt"] += amt
            e["ops"].append((waits if i == 0 else [], f, (d["sem"], amt)))
        self._stamp(("d_" + dname, d["sem"], d["cnt"], None), reads, writes)

    def wait_all(self, qname, resources):
        e = self.eng[qname]
        waits = self._waits(qname, resources, ())
        if waits:
            e["ops"].append((waits, None, None))

    def emit(self, block):
        def run(ename):
            def body(eng):
                for (waits, fn, inc) in self.eng[ename]["ops"]:
                    for (sem, val) in waits:
                        eng.wait_ge(sem, val)
                    if fn is None:
                        continue
                    ins = fn(eng)
                    if inc is not None:
                        ins.then_inc(inc[0], inc[1])
            return body
        block.sync(run("sync"))
        block.gpsimd(run("gpsimd"))
        block.tensor(run("tensor"))
        block.scalar(run("scalar"))
        block.vector(run("vector"))


def build_nc(nlayers=NLAYERS):
    nc = bass.Bass("TRN2", target_bir_lowering=False)
    dt_in = lambda name, shape: nc.dram_tensor(name, shape, F32, kind="ExternalInput").ap()
    xT = dt_in("xT", [D, NT])
    prm_d = dt_in("prm", [128, NP])
    w_in_d = dt_in("w_in", [NLAYERS, D, 1536])
    pool_w_d = dt_in("pool_w", [NLAYERS, 4, 128, 128])
    gr_d = dt_in("gate_r_w", [NLAYERS, 8, 64, 64])
    gi_d = dt_in("gate_i_w", [NLAYERS, 8, 64, 64])
    w_out_d = dt_in("w_out", [NLAYERS, D, D])
    w_up_d = dt_in("w_up", [NLAYERS, D, DFF])
    w_down_d = dt_in("w_down", [NLAYERS, DFF, D])
    outT = nc.dram_tensor("outT", [D, NMAIN], F32, kind="ExternalOutput").ap()
    halo_src = [nc.dram_tensor(f"halo_src{l}", [128, 128], F32) for l in range(nlayers)]
    halo_dst = [nc.dram_tensor(f"halo_dst{l}", [512, 128], F32) for l in range(nlayers)]
    sum_src = [nc.dram_tensor(f"sum_src{l}", [128, 8], F32) for l in range(nlayers)]
    sum_dst = [nc.dram_tensor(f"sum_dst{l}", [512, 8], F32) for l in range(nlayers)]
    groups = [[0, 1, 2, 3], [4, 5, 6, 7]]

    with ExitStack() as st:
        def sb(name, shape, dt):
            return st.enter_context(nc.sbuf_tensor(name, shape, dt))

        P = Prog(nc, st)
        H = sb("H", [128, 8, NT], F32)
        HG = sb("HG", [128, 4, NT], F32)
        XNv = HG[:].bitcast(BF16)
        PG = sb("PG", [128, 4, NMAIN], BF16)
        YPn = sb("YPn", [128, 4, NT], BF16)
        W1 = sb("W1", [128, 8, 1536], BF16)
        RING = [sb(f"ring{i}", [128, 8192], BF16) for i in range(2)]
        RINGF = [r[:].bitcast(F32) for r in RING]
        S = [RINGF[j // 7][:, (j % 7) * 528:(j % 7 + 1) * 528] for j in range(14)]
        prm = sb("prm_sb", [128, NP], F32)
        prd = sb("prd", [128, NLAYERS * 48 + 8], F32)
        ones = sb("ones", [128, 128], BF16)
        pw = sb("pw", [128, 4, 128], BF16)
        grw = sb("grw", [128, 4, 128], BF16)
        giw = sb("giw", [128, 4, 128], BF16)
        sq = sb("sq", [128, 2, 512], BF16)
        xb = sb("xb", [128, 8, 512], BF16)
        hid = [xb[:, 0:4, :], xb[:, 4:8, :]]
        rstd = sb("rstd", [128, 512], F32)
        pooled = sb("pooled", [128, 2, 512], BF16)
        xcb = sb("xcb", [128, 512], BF16)
        UPh = sb("UPh", [128, 4, 16], F32)
        URh = sb("URh", [128, 4, 16], F32)
        state = sb("state", [128, 8], F32)
        hsrc = sb("hsrc", [128, 8, 16], F32)
        hb = sb("hb", [128, 8, 16], F32)
        srecv = sb("srecv", [128, 4, 8], F32)
        carry = sb("carry", [128, 4], F32)
        ctmp = sb("ctmp", [128, 4], F32)
        ps = [st.enter_context(nc.psum_tensor(f"ps{i}", [128, 512], F32)) for i in range(8)]

        R = {}

        def res(name):
            if name not in R:
                R[name] = Res(name)
            return R[name]

        def transfer(dsts, srcs):
            for d in dsts:
                for s in srcs:
                    if s.w is not None:
                        d.rs.append(s.w)
                    d.rs.extend(s.rs)

        rH = lambda c, t: res(f"H{c}_{t}")
        rS = [res(f"S{j}") for j in range(14)]
        rring = [res("ring0"), res("ring1")]
        rxb = [res(f"xb{c}") for c in range(8)]
        rhid = [res("hid0"), res("hid1")]
        rps = [res(f"ps{i}") for i in range(8)]
        bank_i = [0]

        def bank():
            i = bank_i[0] % 8
            bank_i[0] += 1
            return ps[i], rps[i]

        for n_ in (["ld_prm", "w1", "pw", "grw", "giw", "ring0", "ring1", "halo_o", "halo_cc", "halo_i",
                    "sum_o", "sum_cc", "sum_i"] + [f"ld_x{i}" for i in range(5)] + [f"out{i}" for i in range(8)]):
            P.new_dsem(n_)

        P.dma("sync", I("dma_start", out=prm[:], in_=prm_d), [], [res("prm")], "ld_prm")
        for ti, (c0, n) in enumerate(TILES):
            P.dma("sync", I("dma_start",
                out=H[:, :, c0:c0 + n], in_=xT.rearrange("(c p) t -> p c t", p=128)[:, :, c0:c0 + n]),
                [], [rH(c, ti) for c in range(8)], f"ld_x{ti}")
        P.op("vector", I("memset", ones[:], 1.0), [], [res("ones")])
        P.op("vector", I("memset", grw[:], 0.0), [], [res("grw")])
        P.op("vector", I("memset", giw[:], 0.0), [], [res("giw")])

        rprd = res("prd")
        rprm = res("prm")
        DB = lambda l: l * 48
        for l in range(NLAYERS):
            b0, d0 = l * PL, DB(l)
            P.op("vector", I("tensor_scalar",
                out=prd[:, d0:d0 + 16], in0=prm[:, b0:b0 + 16], scalar1=32.0, scalar2=None, op0=ALU.mult),
                [rprm], [rprd])
            P.op("vector", I("tensor_scalar",
                out=prd[:, d0 + 16:d0 + 24], in0=prm[:, b0 + 16:b0 + 24], scalar1=float(np.sqrt(512.0)),
                scalar2=None, op0=ALU.mult), [rprm], [rprd])
            P.op("vector", I("tensor_scalar",
                out=prd[:, d0 + 24:d0 + 32], in0=prm[:, b0 + 52:b0 + 60], scalar1=-1.0, scalar2=None,
                op0=ALU.mult), [rprm], [rprd])
            P.op("scalar", I("activation",
                out=prd[:, d0 + 40:d0 + 44], in_=prm[:, b0 + 60:b0 + 64], func=AF.Exp, scale=-1.0),
                [rprm], [rprd])
        for l in range(NLAYERS):
            d0 = DB(l)
            P.op("scalar", I("activation",
                out=prd[:, d0 + 44:d0 + 48], in_=prd[:, d0 + 40:d0 + 44], func=AF.Ln, bias=1.0, scale=1.0),
                [rprd], [rprd])
            P.op("vector", I("tensor_scalar",
                out=prd[:, d0 + 32:d0 + 36], in0=prd[:, d0 + 44:d0 + 48], scalar1=-8.0, scalar2=None,
                op0=ALU.mult), [rprd], [rprd])
            P.op("vector", I("tensor_scalar",
                out=prd[:, d0 + 36:d0 + 40], in0=prd[:, d0 + 44:d0 + 48], scalar1=-16.0, scalar2=None,
                op0=ALU.mult), [rprd], [rprd])
        DFIN = NLAYERS * 48
        P.op("vector", I("tensor_scalar",
            out=prd[:, DFIN:DFIN + 8], in0=prm[:, P_FIN:P_FIN + 8], scalar1=32.0, scalar2=None, op0=ALU.mult),
            [rprm], [rprd])

        pcol = lambda j: prm[:, j:j + 1]
        dcol = lambda j: prd[:, j:j + 1]
        sq_i = [0]

        def rms_stats(src_fn, nchunks, n, src_res, add_const):
            pt, pr = bank()
            for c in range(nchunks):
                j = sq_i[0] % 2
                sq_i[0] += 1
                P.op("scalar", I("activation", out=sq[:, j, :n], in_=src_fn(c), func=AF.Square),
                     [src_res(c)], [res(f"sq{j}")])
                P.op("tensor", I("matmul", out=pt[:, :n], lhsT=ones[:], rhs=sq[:, j, :n],
                                                           start=(c == 0), stop=(c == nchunks - 1)),
                     [res("ones"), res(f"sq{j}")], [pr])
            P.op("vector", I("tensor_scalar", out=rstd[:, :n], in0=pt[:, :n], scalar1=float(add_const),
                             scalar2=None, op0=ALU.add), [pr], [res("rstd")])
            P.op("scalar", I("activation", out=rstd[:, :n], in_=rstd[:, :n], func=AF.Ln), [res("rstd")], [res("rstd")])
            P.op("scalar", I("activation", out=rstd[:, :n], in_=rstd[:, :n], func=AF.Exp, scale=-0.5),
                 [res("rstd")], [res("rstd")])

        def load_w1(src3, ncols):
            P.dma("gpsimd", [I("dma_start", out=W1[:, k:k + 2, :ncols], in_=src3[:, k:k + 2, :])
                             for k in range(0, 8, 2)], [], [res("W1")], "w1")

        def load_small(l):
            P.dma("gpsimd", I("dma_start", out=pw[:], in_=pool_w_d[l].rearrange("g i j -> i g j")),
                  [], [res("pw")], "pw")
            for (dst, srcd, rn) in ((grw, gr_d, "grw"), (giw, gi_d, "giw")):
                s4 = srcd[l].rearrange("(q two) i j -> two i q j", two=2)
                P.dma("gpsimd", [I("dma_start", out=dst[64 * b:64 * b + 64, :, 64 * b:64 * b + 64], in_=s4[b])
                                 for b in range(2)], [], [res(rn)], rn)

        def inproj_norm(c0, n, ti, gbase):
            rms_stats(lambda c: H[:, c, c0:c0 + n], 8, n, lambda c: rH(c, ti), D * EPS)
            for c in range(8):
                P.op("vector", I("scalar_tensor_tensor",
                    out=xb[:, c, :n], in0=H[:, c, c0:c0 + n], scalar=dcol(gbase + c), in1=rstd[:, :n],
                    op0=ALU.mult, op1=ALU.mult), [rH(c, ti), res("rstd"), rprd], [rxb[c]])

        def inproj_mm(m, n):
            pt, pr = bank()
            P.group("tensor", [
                (I("matmul", out=pt[:, :n], lhsT=W1[:, kc, m * 128:(m + 1) * 128], rhs=xb[:, kc, :n],
                                           start=(kc == 0), stop=(kc == 7)))
                for kc in range(8)], [res("W1")] + rxb, [pr])
            return pt, pr

        rHG = lambda m, ti: res(f"HG{m}_{ti}")
        rXN = lambda c, ti: res(f"XN{c}_{ti}")
        allHG = [rHG(m, ti) for m in range(4) for ti in range(5)]
        allXN = [rXN(c, ti) for c in range(8) for ti in range(5)]
        XN = lambda c, c0, n: XNv[:, c // 2, (c % 2) * NT + c0:(c % 2) * NT + c0 + n]

        load_w1(w_in_d[0].rearrange("(k p) n -> p k n", p=128), 1536)
        for l in range(nlayers):
            b0, d0 = l * PL, DB(l)
            load_small(l)
            if l > 0:
                for i in range(2):
                    transfer(rS[7 * i:7 * i + 7], [rring[i]])
                transfer(rxb[0:4], [rhid[0]])
                transfer(rxb[4:8], [rhid[1]])
                transfer(allHG, allXN)
            P.op("vector", I("memset", state[:, 0:4], 0.0), [], [res("state")])
            P.op("vector", I("memset", state[:, 4:8], 1.0), [], [res("state")])
            P.op("vector", I("memset", UPh[:], 0.0), [], [res("UPh")])
            P.op("vector", I("memset", URh[:], 0.0), [], [res("URh")])

            c0t = NT - 16
            inproj_norm(c0t, 16, 4, d0 + 0)
            for m in range(8):
                pt, pr = inproj_mm(m, 16)
                P.op("scalar", I("copy", out=hsrc[:, m, :], in_=pt[:, :16]), [pr], [res("hsrc")])
            P.dma("sync", I("dma_start", out=halo_src[l].ap(), in_=hsrc[:].rearrange("p m j -> p (m j)")),
                  [res("hsrc")], [res(f"halo_src{l}")], "halo_o")
            P.dma("gpsimd", I("collective_compute",
                "AllGather", ALU.bypass, replica_groups=groups, ins=[halo_src[l].ap()], outs=[halo_dst[l].ap()]),
                [res(f"halo_src{l}")], [res(f"halo_dst{l}")], "halo_cc", amt=1)
            hrecv = S[6][:, 0:512].rearrange("p (r f) -> p r f", r=4)
            P.dma("sync", I("dma_start",
                out=hrecv, in_=halo_dst[l].ap().rearrange("(r p) f -> p r f", p=128)),
                [res(f"halo_dst{l}")], [rS[6]], "halo_i")
            hbf = hb[:].rearrange("p m j -> p (m j)")
            P.op("vector", I("tensor_scalar", out=hbf, in0=hrecv[:, 0, :], scalar1=pcol(P_SEL), scalar2=None,
                                                     op0=ALU.mult), [rS[6], rprm], [res("hb")])
            for r in range(1, 4):
                P.op("vector", I("scalar_tensor_tensor",
                    out=hbf, in0=hrecv[:, r, :], scalar=pcol(P_SEL + r), in1=hbf, op0=ALU.mult, op1=ALU.add),
                    [rS[6], res("hb"), rprm], [res("hb")])

            for ti, (c0, n) in enumerate(TILES):
                is_pre = (ti == 0)
                W = 16 + n
                inproj_norm(c0, n, ti, d0 + 0)
                for m in range(4):
                    k = 2 << m
                    pt, pr = inproj_mm(m, n)
                    P.op("vector", I("tensor_copy", out=S[0][:, 0:16], in_=UPh[:, m, :]),
                         [res("UPh")], [rS[0]])
                    P.op("scalar", I("copy", out=S[0][:, 16:W], in_=pt[:, :n]), [pr], [rS[0]])
                    if is_pre:
                        P.op("vector", I("scalar_tensor_tensor",
                            out=S[0][:, 16:32], in0=S[0][:, 16:32], scalar=pcol(P_M0), in1=hb[:, m, :],
                            op0=ALU.mult, op1=ALU.add), [rS[0], res("hb"), rprm], [rS[0]])
                    P.op("vector", I("tensor_copy", out=UPh[:, m, :], in_=S[0][:, n:W]),
                         [rS[0]], [res("UPh")])
                    cur, curres = 0, rS[0]
                    step = 1
                    bi = 1
                    while step < k:
                        lo = 2 * step - 1
                        P.op("vector", I("tensor_tensor",
                            out=S[bi][:, lo:W], in0=S[cur][:, lo:W], in1=S[cur][:, lo - step:W - step], op=ALU.add),
                            [curres], [rS[bi]])
                        cur, curres = bi, rS[bi]
                        step *= 2
                        bi = 3 - bi
                    pj = m % 2
                    if is_pre:
                        o = 3 - cur
                        P.op("vector", I("tensor_tensor",
                            out=S[o][:, 0:16], in0=S[cur][:, 16:32],
                            in1=prm[:, P_INVC + 16 * m:P_INVC + 16 * m + 16], op=ALU.mult),
                            [curres, rprm], [rS[o]])
                        P.op("vector", I("tensor_tensor",
                            out=pooled[:, pj, :16], in0=S[o][:, 0:16], in1=S[0][:, 16:32], op=ALU.subtract),
                            [rS[o], rS[0]], [res(f"pooled{pj}")])
                    else:
                        P.op("vector", I("scalar_tensor_tensor",
                            out=pooled[:, pj, :n], in0=S[cur][:, 16:W], scalar=1.0 / k, in1=S[0][:, 16:W],
                            op0=ALU.mult, op1=ALU.subtract), [curres, rS[0]], [res(f"pooled{pj}")])
                    pt2, pr2 = bank()
                    P.op("tensor", I("matmul",
                        out=pt2[:, :n], lhsT=pw[:, m, :], rhs=pooled[:, pj, :n], start=True, stop=True),
                        [res("pw"), res(f"pooled{pj}")], [pr2])
                    P.op("vector", I("tensor_scalar",
                        out=S[3 + m][:, :n], in0=pt2[:, :n], scalar1=pcol(b0 + 24 + m), scalar2=pcol(b0 + 28 + m),
                        op0=ALU.add, op1=ALU.mult), [pr2, rprm], [rS[3 + m]])
                rms_stats(lambda c: S[3 + c][:, :n], 4, n, lambda c: rS[3 + c], 512 * EPS)
                for m in range(4):
                    P.op("vector", I("scalar_tensor_tensor",
                        out=YPn[:, m, c0:c0 + n], in0=S[3 + m][:, :n], scalar=dcol(d0 + 16 + m), in1=rstd[:, :n],
                        op0=ALU.mult, op1=ALU.mult), [rS[3 + m], res("rstd"), rprd], [res(f"YPn{m}_{ti}")])

                for m in range(4):
                    pt, pr = inproj_mm(4 + m, n)
                    P.op("vector", I("tensor_copy", out=S[7][:, 0:16], in_=URh[:, m, :]),
                         [res("URh")], [rS[7]])
                    P.op("scalar", I("copy", out=S[7][:, 16:W], in_=pt[:, :n]), [pr], [rS[7]])
                    if is_pre:
                        P.op("vector", I("scalar_tensor_tensor",
                            out=S[7][:, 16:32], in0=S[7][:, 16:32], scalar=pcol(P_M0), in1=hb[:, 4 + m, :],
                            op0=ALU.mult, op1=ALU.add), [rS[7], res("hb"), rprm], [rS[7]])
                    P.op("vector", I("tensor_copy", out=URh[:, m, :], in_=S[7][:, n:W]),
                         [rS[7]], [res("URh")])
                    ptg, prg = inproj_mm(8 + m, n)
                    P.op("vector", I("tensor_copy", out=S[9][:, :n], in_=ptg[:, :n]), [prg], [rS[9]])
                    P.op("scalar", I("activation", out=S[10][:, :n], in_=ptg[:, :n], func=AF.Square),
                         [prg], [rS[10]])
                    def sigmoid_chain(dst, dres, src_ap, sres, scale, bias):
                        P.op("scalar", I("activation", out=dst, in_=src_ap, func=AF.Exp, bias=bias, scale=scale),
                             sres + [rprd], [dres])
                        P.op("scalar", I("activation", out=dst, in_=dst, func=AF.Ln, bias=1.0, scale=1.0),
                             [dres], [dres])
                        P.op("scalar", I("activation", out=dst, in_=dst, func=AF.Exp, scale=-1.0), [dres], [dres])
                    P.op("vector", I("tensor_scalar",
                        out=S[10][:, :n], in0=S[10][:, :n], scalar1=GELU_C2, scalar2=1.0, op0=ALU.mult, op1=ALU.add),
                        [rS[10]], [rS[10]])
                    P.op("vector", I("tensor_tensor",
                        out=S[10][:, :n], in0=S[10][:, :n], in1=S[9][:, :n], op=ALU.mult),
                        [rS[10], rS[9]], [rS[10]])
                    sigmoid_chain(S[10][:, :n], rS[10], S[10][:, :n], [rS[10]], -2.0 * GELU_C1, 0.0)
                    P.op("vector", I("tensor_tensor", out=S[10][:, :n], in0=S[10][:, :n], in1=S[9][:, :n],
                                     op=ALU.mult), [rS[10], rS[9]], [rS[10]])
                    P.op("vector", I("tensor_scalar",
                        out=S[8][:, :n], in0=S[7][:, 13:13 + n], scalar1=pcol(b0 + 32 + m), scalar2=pcol(b0 + 48 + m),
                        op0=ALU.mult, op1=ALU.add), [rS[7], rprm], [rS[8]])
                    for k in range(1, 4):
                        P.op("vector", I("scalar_tensor_tensor",
                            out=S[8][:, :n], in0=S[7][:, 13 + k:13 + k + n], scalar=pcol(b0 + 32 + 4 * k + m),
                            in1=S[8][:, :n], op0=ALU.mult, op1=ALU.add), [rS[7], rS[8], rprm], [rS[8]])
                    P.op("scalar", I("copy", out=xcb[:, :n], in_=S[8][:, :n]), [rS[8]], [res("xcb")])
                    ptr, prr = bank()
                    P.op("tensor", I("matmul", out=ptr[:, :n], lhsT=grw[:, m, :],
                                                                   rhs=xcb[:, :n], start=True, stop=True),
                         [res("grw"), res("xcb")], [prr])
                    pti, pri = bank()
                    P.op("tensor", I("matmul", out=pti[:, :n], lhsT=giw[:, m, :],
                                                                   rhs=xcb[:, :n], start=True, stop=True),
                         [res("giw"), res("xcb")], [pri])
                    def sigmoid_chain(dst, dres, src_ap, sres, scale, bias):
                        P.op("scalar", I("activation", out=dst, in_=src_ap, func=AF.Exp, bias=bias, scale=scale),
                             sres + [rprd], [dres])
                        P.op("scalar", I("activation", out=dst, in_=dst, func=AF.Ln, bias=1.0, scale=1.0),
                             [dres], [dres])
                        P.op("scalar", I("activation", out=dst, in_=dst, func=AF.Exp, scale=-1.0), [dres], [dres])
                    sigmoid_chain(S[11][:, :n], rS[11], ptr[:, :n], [prr], -1.0, dcol(d0 + 24 + m))
                    sigmoid_chain(S[12][:, :n], rS[12], pti[:, :n], [pri], -1.0, dcol(d0 + 28 + m))
                    P.op("scalar", I("activation", out=S[1][:, :n], in_=S[11][:, :n], func=AF.Exp,
                                     scale=dcol(d0 + 32 + m)), [rS[11], rprd], [rS[1]])
                    P.op("scalar", I("activation", out=S[13][:, :n], in_=S[11][:, :n], func=AF.Exp,
                                     scale=dcol(d0 + 36 + m)), [rS[11], rprd], [rS[13]])
                    P.op("vector", I("tensor_scalar", out=S[13][:, :n], in0=S[13][:, :n], scalar1=1.0, scalar2=-1.0,
                                     op0=ALU.min, op1=ALU.mult), [rS[13]], [rS[13]])
                    P.op("scalar", I("activation", out=S[13][:, :n], in_=S[13][:, :n], func=AF.Ln, bias=1.0,
                                     scale=1.0), [rS[13]], [rS[13]])
                    P.op("scalar", I("activation", out=S[13][:, :n], in_=S[13][:, :n], func=AF.Exp, scale=0.5),
                         [rS[13]], [rS[13]])
                    P.op("vector", I("tensor_tensor", out=S[12][:, :n], in0=S[12][:, :n], in1=S[8][:, :n],
                                     op=ALU.mult), [rS[12], rS[8]], [rS[12]])
                    P.op("vector", I("tensor_tensor", out=S[12][:, :n], in0=S[13][:, :n], in1=S[12][:, :n],
                                     op=ALU.mult), [rS[13], rS[12]], [rS[12]])
                    P.op("vector", I("tensor_tensor_scan",
                        out=S[2][:, :n], data0=S[1][:, :n], data1=S[12][:, :n], initial=state[:, m:m + 1],
                        op0=ALU.mult, op1=ALU.add), [rS[1], rS[12], res("state")], [rS[2]])
                    if is_pre:
                        P.op("vector", I("tensor_scalar",
                            out=state[:, m:m + 1], in0=S[2][:, n - 1:n], scalar1=pcol(P_M0), scalar2=None, op0=ALU.mult),
                            [rS[2], rprm], [res("state")])
                    else:
                        P.op("vector", I("tensor_copy", out=state[:, m:m + 1], in_=S[2][:, n - 1:n]),
                             [rS[2]], [res("state")])
                        P.op("vector", I("tensor_tensor_scan",
                            out=S[0][:, :n], data0=S[1][:, :n], data1=S[1][:, :n], initial=state[:, 4 + m:5 + m],
                            op0=ALU.mult, op1=ALU.min), [rS[1], res("state")], [rS[0]])
                        P.op("vector", I("tensor_copy", out=state[:, 4 + m:5 + m], in_=S[0][:, n - 1:n]),
                             [rS[0]], [res("state")])
                    P.op("vector", I("tensor_tensor", out=HG[:, m, c0:c0 + n], in0=S[2][:, :n], in1=S[10][:, :n],
                                     op=ALU.mult), [rS[2], rS[10]], [rHG(m, ti)])
                    if not is_pre:
                        P.op("vector", I("tensor_tensor", out=PG[:, m, c0 - 16:c0 - 16 + n], in0=S[0][:, :n],
                                         in1=S[10][:, :n], op=ALU.mult), [rS[0], rS[10]], [res(f"PG{m}_{ti}")])

            P.dma("sync", I("dma_start", out=sum_src[l].ap(), in_=state[:]),
                  [res("state")], [res(f"sum_src{l}")], "sum_o")
            P.dma("gpsimd", I("collective_compute",
                "AllGather", ALU.bypass, replica_groups=groups, ins=[sum_src[l].ap()], outs=[sum_dst[l].ap()]),
                [res(f"sum_src{l}")], [res(f"sum_dst{l}")], "sum_cc", amt=1)
            P.dma("sync", I("dma_start",
                out=srecv[:], in_=sum_dst[l].ap().rearrange("(r p) f -> p r f", p=128)),
                [res(f"sum_dst{l}")], [res("srecv")], "sum_i")
            load_w1(w_out_d[l].rearrange("(k p) n -> p k n", p=128), 1024)
            P.op("vector", I("memset", carry[:], 0.0), [], [res("carry")])
            for r in range(3):
                P.op("vector", I("tensor_tensor", out=ctmp[:], in0=srecv[:, r, 4:8], in1=carry[:],
                                                            op=ALU.mult), [res("srecv"), res("carry")], [res("ctmp")])
                P.op("vector", I("tensor_tensor", out=ctmp[:], in0=ctmp[:], in1=srecv[:, r, 0:4],
                                                            op=ALU.add), [res("srecv"), res("ctmp")], [res("ctmp")])
                P.op("vector", I("tensor_tensor", out=ctmp[:], in0=ctmp[:], in1=carry[:],
                                                            op=ALU.subtract), [res("carry"), res("ctmp")], [res("ctmp")])
                P.op("vector", I("scalar_tensor_tensor",
                    out=carry[:], in0=ctmp[:], scalar=pcol(P_MPREV + r), in1=carry[:], op0=ALU.mult, op1=ALU.add),
                    [res("ctmp"), res("carry"), rprm], [res("carry")])

            ynv = [S[4].bitcast(BF16), S[5].bitcast(BF16)]
            YN = lambda m, n: ynv[m // 2][:, (m % 2) * 512:(m % 2) * 512 + n]
            rYN = lambda m: rS[4 + m // 2]
            for ti, (c0, n) in enumerate(TILES):
                is_pre = (ti == 0)
                for m in range(4):
                    if is_pre:
                        P.op("vector", I("tensor_copy", out=S[m][:, :n], in_=HG[:, m, c0:c0 + n]),
                             [rHG(m, ti)], [rS[m]])
                    else:
                        P.op("vector", I("scalar_tensor_tensor",
                            out=S[m][:, :n], in0=PG[:, m, c0 - 16:c0 - 16 + n], scalar=carry[:, m:m + 1],
                            in1=HG[:, m, c0:c0 + n], op0=ALU.mult, op1=ALU.add),
                            [res(f"PG{m}_{ti}"), rHG(m, ti), res("carry")], [rS[m]])
                rms_stats(lambda c: S[c][:, :n], 4, n, lambda c: rS[c], 512 * EPS)
                for m in range(4):
                    P.op("vector", I("scalar_tensor_tensor",
                        out=YN(m, n), in0=S[m][:, :n], scalar=dcol(d0 + 20 + m), in1=rstd[:, :n],
                        op0=ALU.mult, op1=ALU.mult), [rS[m], res("rstd"), rprd], [rYN(m)])
                for mo in range(8):
                    pt, pr = bank()
                    fns = []
                    for kc in range(8):
                        rhs = YPn[:, kc, c0:c0 + n] if kc < 4 else YN(kc - 4, n)
                        fns.append(I("matmul",
                            out=pt[:, :n], lhsT=W1[:, kc, mo * 128:(mo + 1) * 128], rhs=rhs,
                            start=(kc == 0), stop=(kc == 7)))
                    P.group("tensor", fns, [res("W1")] + [res(f"YPn{m}_{ti}") for m in range(4)] +
                            [rS[4], rS[5]], [pr])
                    P.op("vector", I("tensor_tensor",
                        out=H[:, mo, c0:c0 + n], in0=pt[:, :n], in1=H[:, mo, c0:c0 + n], op=ALU.add),
                        [pr, rH(mo, ti)], [rH(mo, ti)])
            if l + 1 < nlayers:
                load_w1(w_in_d[l + 1].rearrange("(k p) n -> p k n", p=128), 1536)

            transfer(allXN, allHG)
            for ti, (c0, n) in enumerate(TILES):
                rms_stats(lambda c: H[:, c, c0:c0 + n], 8, n, lambda c: rH(c, ti), D * EPS)
                for c in range(8):
                    P.op("vector", I("scalar_tensor_tensor",
                        out=XN(c, c0, n), in0=H[:, c, c0:c0 + n], scalar=dcol(d0 + 8 + c), in1=rstd[:, :n],
                        op0=ALU.mult, op1=ALU.mult), [rH(c, ti), res("rstd"), rprd], [rXN(c, ti)])

            transfer([rhid[0]], rxb[0:4])
            transfer([rhid[1]], rxb[4:8])
            for i in range(2):
                transfer([rring[i]], rS[7 * i:7 * i + 7])
            for g in range(8):
                rg = RING[g % 2]
                rr = rring[g % 2]
                dn = f"ring{g % 2}"
                P.dma("gpsimd", [I("dma_start",
                    out=rg[:, 0:4096].rearrange("p (k f) -> p k f", k=8),
                    in_=w_up_d[l].rearrange("(k p) f -> p k f", p=128)[:, :, g * 512:(g + 1) * 512]),
                    I("dma_start",
                    out=rg[:, 4096:8192].rearrange("p (k n) -> p k n", k=4),
                    in_=w_down_d[l][g * 512:(g + 1) * 512, :].rearrange("(k p) n -> p k n", p=128))],
                    [], [rr], dn)
                for ti, (c0, n) in enumerate(TILES):
                    hj = (g * 5 + ti) % 2
                    hb_ = hid[hj]
                    for fc in range(4):
                        pt, pr = bank()
                        P.group("tensor", [
                            (I("matmul",
                                out=pt[:, :n], lhsT=rg[:, kc * 512 + fc * 128:kc * 512 + (fc + 1) * 128],
                                rhs=XN(kc, c0, n), start=(kc == 0), stop=(kc == 7)))
                            for kc in range(8)], [rr] + [rXN(c, ti) for c in range(8)], [pr])
                        P.op("scalar", I("activation", out=hb_[:, fc, :n], in_=pt[:, :n], func=AF.Relu),
                             [pr], [rhid[hj]])
                        P.op("vector", I("tensor_tensor", out=hb_[:, fc, :n], in0=hb_[:, fc, :n],
                                         in1=hb_[:, fc, :n], op=ALU.mult), [rhid[hj]], [rhid[hj]])
                    for mo in range(8):
                        pt, pr = bank()
                        P.group("tensor", [
                            (I("matmul",
                                out=pt[:, :n], lhsT=rg[:, 4096 + fc * 1024 + mo * 128:4096 + fc * 1024 + (mo + 1) * 128],
                                rhs=hb_[:, fc, :n], start=(fc == 0), stop=(fc == 3)))
                            for fc in range(4)], [rr, rhid[hj]], [pr])
                        P.op("vector", I("tensor_tensor",
                            out=H[:, mo, c0:c0 + n], in0=pt[:, :n], in1=H[:, mo, c0:c0 + n], op=ALU.add),
                            [pr, rH(mo, ti)], [rH(mo, ti)])

        for i in range(2):
            transfer(rS[7 * i:7 * i + 7], [rring[i]])
        outv = outT.rearrange("(c p) t -> p c t", p=128)
        for ti, (c0, n) in enumerate(TILES):
            if ti == 0:
                continue
            rms_stats(lambda c: H[:, c, c0:c0 + n], 8, n, lambda c: rH(c, ti), D * EPS)
            for c in range(8):
                P.op("vector", I("scalar_tensor_tensor",
                    out=S[c][:, :n], in0=H[:, c, c0:c0 + n], scalar=dcol(DFIN + c), in1=rstd[:, :n],
                    op0=ALU.mult, op1=ALU.mult), [rH(c, ti), res("rstd"), rprd], [rS[c]])
                P.dma("sync", I("dma_start",
                    out=outv[:, c, c0 - 16:c0 - 16 + n], in_=S[c][:, :n]),
                    [rS[c]], [res(f"outT{c}")], f"out{c}")
        P.wait_all("sync", [res(f"outT{c}") for c in range(8)])

        with nc.Block() as block:
            P.emit(block)
    return nc


def _prep_inputs(inputs):
    f = lambda k: np.ascontiguousarray(np.asarray(inputs[k], dtype=np.float32))
    x = f("x")
    meta = f("meta_tokens")
    pcols = np.zeros((128, NP), np.float32)
    chunks = lambda v: v.reshape(-1, 128).T
    for l in range(NLAYERS):
        b0 = l * PL
        pcols[:, b0 + 0:b0 + 8] = chunks(f("mix_norm_g")[l])
        pcols[:, b0 + 8:b0 + 16] = chunks(f("mlp_norm_g")[l])
        pcols[:, b0 + 16:b0 + 24] = chunks(f("group_norm_g")[l])
        pcols[:, b0 + 24:b0 + 28] = chunks(f("pool_b")[l])
        pcols[:, b0 + 28:b0 + 32] = chunks(f("pool_scale")[l])
        for k in range(4):
            pcols[:, b0 + 32 + 4 * k:b0 + 36 + 4 * k] = chunks(f("conv_w")[l, k])
        pcols[:, b0 + 48:b0 + 52] = chunks(f("conv_b")[l])
        pcols[:, b0 + 52:b0 + 56] = chunks(f("gate_r_b")[l])
        pcols[:, b0 + 56:b0 + 60] = chunks(f("gate_i_b")[l])
        pcols[:, b0 + 60:b0 + 64] = chunks(f("lru_lambda")[l])
    pcols[:, P_FIN:P_FIN + 8] = chunks(f("final_norm_g"))
    in_maps = []
    shared = {k: f(k) for k in ("w_in", "pool_w", "gate_r_w", "gate_i_w", "w_out", "w_up", "w_down")}
    for core in range(8):
        b, c = core // 4, core % 4
        xT = np.zeros((D, NT), np.float32)
        if c == 0:
            xT[:, :NPRE] = meta.T
        xT[:, NPRE:] = x[b, c * NMAIN:(c + 1) * NMAIN, :].T
        pc = pcols.copy()
        if c > 0:
            pc[:, P_SEL + c - 1] = 1.0
        for r in range(4):
            pc[:, P_MPREV + r] = 1.0 if r < c else 0.0
        pc[:, P_M0] = 1.0 if c == 0 else 0.0
        for g in range(4):
            k = 2 << g
            for t in range(16):
                pc[:, P_INVC + 16 * g + t] = 1.0 / (min(t + 1, k) if c == 0 else k)
        m = {"xT": xT, "prm": pc}
        m.update(shared)
        in_maps.append(m)
    return in_maps


_NC_CACHE = {}


def kernel(**inputs):
    nl = NLAYERS
    if nl not in _NC_CACHE:
        _NC_CACHE[nl] = build_nc(nl)
    nc = _NC_CACHE[nl]
    in_maps = _prep_inputs(inputs)
    res = run_bass_kernel_spmd(nc, in_maps, core_ids=list(range(8)))
    out = np.empty((2, 4 * NMAIN, D), np.float32)
    for core in range(8):
        b, c = core // 4, core % 4
        out[b, c * NMAIN:(c + 1) * NMAIN, :] = res.results[core]["outT"].T
    return out
```
